# Optimizing a Trainium2 kernel written in Bass

```python
import jax, jax.numpy as jnp
from jax import lax
import numpy as np

D_MODEL = 1024
BATCH = 8
SEQ = 4096
DEPTH = 4

N_A = DEPTH // 2
N_B = DEPTH - N_A
POOL_WINDOWS = (2, 4, 8, 16)
POOL_GROUPS = len(POOL_WINDOWS)
GC = D_MODEL // POOL_GROUPS
N_HEADS = 16
HEAD_DIM = D_MODEL // N_HEADS
D_FF = 4 * D_MODEL
Q_BLOCK = 128
EPS = 1e-6

kernel_name = "yoco_pool_stickbreak_trunk"


def rms_norm(x, g):
    xf = x.astype(jnp.float32)
    y = xf * lax.rsqrt(jnp.mean(xf * xf, axis=-1, keepdims=True) + EPS)
    return (y * g.astype(jnp.float32)).astype(x.dtype)


def pool_mixer(h, w_grp, scale):
    B, S, D = h.shape
    hf = h.astype(jnp.float32)
    cp = jnp.concatenate([jnp.zeros((B, 1, D), jnp.float32), lax.cumsum(hf, axis=1)], axis=1)
    pos = jnp.arange(S, dtype=jnp.int32)
    outs = []
    for g, w in enumerate(POOL_WINDOWS):
        sl = slice(g * GC, (g + 1) * GC)
        upper = cp[:, 1:, sl]
        lower = jnp.concatenate([jnp.zeros((B, w - 1, GC), jnp.float32), cp[:, :S + 1 - w, sl]], axis=1)
        count = jnp.minimum(pos + 1, w).astype(jnp.float32)[None, :, None]
        outs.append((upper - lower) / count - hf[:, :, sl])
    y = jnp.stack(outs, axis=2)
    y = jnp.einsum('bsgc,gcd->bsgd', y, w_grp.astype(jnp.float32)).reshape(B, S, D)
    return (y * scale.astype(jnp.float32)).astype(h.dtype)


def stick_breaking_attention(q, k, v):
    S = q.shape[2]
    inv_sqrt_d = 1.0 / float(np.sqrt(HEAD_DIM))
    outs = []
    for i in range(S // Q_BLOCK):
        kv_len = (i + 1) * Q_BLOCK
        q_blk = q[:, :, i * Q_BLOCK:kv_len]
        k_blk = k[:, :, :kv_len]
        v_blk = v[:, :, :kv_len]
        z = jnp.einsum('bhqd,bhkd->bhqk', q_blk, k_blk,
                       preferred_element_type=jnp.float32) * inv_sqrt_d
        t_idx = i * Q_BLOCK + jnp.arange(Q_BLOCK, dtype=jnp.int32)[:, None]
        s_idx = jnp.arange(kv_len, dtype=jnp.int32)[None, :]
        mask = s_idx < t_idx
        log1m = jnp.where(mask, -jax.nn.softplus(z), 0.0)
        excl = lax.cumsum(log1m, axis=3, reverse=True) - log1m
        a = jnp.where(mask, jnp.exp(jax.nn.log_sigmoid(z) + excl), 0.0)
        outs.append(jnp.einsum('bhqk,bhkd->bhqd', a, v_blk.astype(jnp.float32)))
    return jnp.concatenate(outs, axis=2).astype(q.dtype)


def sq_relu_mlp(h, w_up, w_down):
    u = jnp.matmul(h, w_up)
    return jnp.matmul(jnp.square(jax.nn.relu(u)), w_down)


def setup_inputs(seed: int = 0) -> dict:
    key = jax.random.key(seed)
    ks = jax.random.split(key, 16)
    D, HD = D_MODEL, N_HEADS * HEAD_DIM
    nrm = lambda k, shape, fan_in: jax.random.normal(k, shape, jnp.float32) * (fan_in ** -0.5)
    gain = lambda k, shape: 1.0 + 0.05 * jax.random.normal(k, shape, jnp.float32)
    return {
        "x": jax.random.normal(ks[0], (BATCH, SEQ, D), jnp.float32),
        "pool_w": nrm(ks[1], (N_A, POOL_GROUPS, GC, GC), GC),
        "pool_scale": 0.5 + jax.random.uniform(ks[2], (N_A, D), jnp.float32),
        "w_q": nrm(ks[3], (N_B, D, HD), D),
        "w_kv": nrm(ks[4], (D, 2 * HD), D),
        "kv_norm_g": gain(ks[5], (D,)),
        "w_o": nrm(ks[6], (N_B, HD, D), HD),
        "w_up": nrm(ks[7], (DEPTH, D, D_FF), D),
        "w_down": nrm(ks[8], (DEPTH, D_FF, D), D_FF),
        "mix_pre_g": gain(ks[9], (DEPTH, D)),
        "mix_post_g": gain(ks[10], (DEPTH, D)),
        "mlp_pre_g": gain(ks[11], (DEPTH, D)),
        "mlp_post_g": gain(ks[12], (DEPTH, D)),
    }


def reference(x, pool_w, pool_scale, w_q, w_kv, kv_norm_g, w_o, w_up, w_down,
              mix_pre_g, mix_post_g, mlp_pre_g, mlp_post_g):
    B, S, D = x.shape
    HD = N_HEADS * HEAD_DIM
    k = v = None
    for layer in range(DEPTH):
        h = rms_norm(x, mix_pre_g[layer])
        if layer < N_A:
            m = pool_mixer(h, pool_w[layer], pool_scale[layer])
        else:
            j = layer - N_A
            q = jnp.matmul(h, w_q[j]).reshape(B, S, N_HEADS, HEAD_DIM).transpose(0, 2, 1, 3)
            o = stick_breaking_attention(q, k, v)
            m = jnp.matmul(o.transpose(0, 2, 1, 3).reshape(B, S, HD), w_o[j])
        x = x + rms_norm(m, mix_post_g[layer])
        h = rms_norm(x, mlp_pre_g[layer])
        x = x + rms_norm(sq_relu_mlp(h, w_up[layer], w_down[layer]), mlp_post_g[layer])
        if layer == N_A - 1:
            kv = jnp.matmul(rms_norm(x, kv_norm_g), w_kv).reshape(B, S, 2, N_HEADS, HEAD_DIM)
            k = kv[:, :, 0].transpose(0, 2, 1, 3)
            v = kv[:, :, 1].transpose(0, 2, 1, 3)
    return x
```

```python
import numpy as np
from contextlib import ExitStack
import concourse.bass as bass
import concourse.mybir as mybir
from concourse.bass_utils import run_bass_kernel_spmd

F32 = mybir.dt.float32
BF16 = mybir.dt.bfloat16
ALU = mybir.AluOpType
AF = mybir.ActivationFunctionType

D = 1024
KC = 8
FF = 4096
FC = 32
EPS = 1e-6
POOL_WINDOWS = (2, 4, 8, 16)
ENGS = (("pe", "tensor"), ("act", "scalar"), ("dve", "vector"), ("pool", "gpsimd"), ("sp", "sync"))


class Sem:
    def __init__(self, h):
        self.h = h
        self.n = 0
        self.bn = 0


class Ring:
    def __init__(self, tiles):
        self.tiles = tiles
        self.rel = [[] for _ in tiles]
        self.k = 0

    def acquire(self):
        i = self.k % len(self.tiles)
        self.k += 1
        w = self.rel[i]
        self.rel[i] = []
        return i, self.tiles[i], w

    def release(self, i, *toks):
        self.rel[i].extend(t for t in toks if t is not None)


class Sched:
    def __init__(self, nc, es):
        self.nc = nc
        self.es = es
        self.sems = []
        self.q = {e: [] for e, _ in ENGS}
        self.esem = {e: self.newsem("s_" + e) for e, _ in ENGS}
        self.pending = {e: [] for e, _ in ENGS}

    def newsem(self, name):
        s = Sem(self.es.enter_context(self.nc.semaphore(name)))
        self.sems.append(s)
        return s

    def add(self, eng, fn, waits=(), sem=None, dma=False):
        if dma:
            assert sem is not None
        s = sem if sem is not None else self.esem[eng]
        amt = 16 if dma else 1
        s.n += amt
        ws = [w for w in waits if w is not None]
        if self.pending[eng]:
            ws = self.pending[eng] + ws
            self.pending[eng] = []
        self.q[eng].append((fn, ws, s, amt))
        return (s, s.n)

    def barrier(self):
        toks = [(s, s.n) for s in self.sems if s.n > s.bn]
        for s in self.sems:
            s.bn = s.n
        for e, _ in ENGS:
            self.pending[e] = self.pending[e] + toks

    def emit(self):
        with self.nc.Block() as block:
            for name, attr in ENGS:
                ops = self.q[name]

                def body(eng, ops=ops):
                    seen = {}
                    for fn, waits, s, amt in ops:
                        for (ws, wv) in waits:
                            if seen.get(id(ws), 0) < wv:
                                eng.wait_ge(ws.h, wv)
                                seen[id(ws)] = wv
                        ins = fn(eng)
                        ins.then_inc(s.h, amt)

                getattr(block, attr)(body)
        self.q = {e: [] for e, _ in ENGS}


def build(S):
    NT = S // 128
    assert S % 512 == 0
    nc = bass.Bass("TRN2", target_bir_lowering=False)
    dr = lambda n, s, d, k: nc.dram_tensor(n, s, d, kind=k).ap()
    x_in = dr("x", [S, D], F32, "ExternalInput")
    gains = dr("gains", [19, D], F32, "ExternalInput")
    pool_w = dr("pool_w", [2, 4, 256, 256], F32, "ExternalInput")
    w_q = dr("w_q", [2, D, D], F32, "ExternalInput")
    w_kv = dr("w_kv", [D, 2 * D], F32, "ExternalInput")
    w_o = dr("w_o", [2, D, D], F32, "ExternalInput")
    w_up = dr("w_up", [4, D, FF], F32, "ExternalInput")
    w_down = dr("w_down", [4, FF, D], F32, "ExternalInput")
    c_ident = dr("c_ident", [128, 128], F32, "ExternalInput")
    c_maskfirst = dr("c_maskfirst", [128, 1024], F32, "ExternalInput")
    c_maskT = dr("c_maskT", [128, 128], F32, "ExternalInput")
    c_maskQ = dr("c_maskQ", [128, 128], F32, "ExternalInput")
    c_poolB = dr("c_poolB", [12, 128, 128], F32, "ExternalInput")
    out = dr("out", [S, D], F32, "ExternalOutput")
    xs = dr("xs", [S, D], F32, "Internal")
    Vd = dr("Vd", [S + 1, D], F32, "Internal")
    dVd = dr("dVd", [S, D], BF16, "Internal")
    KTd = dr("KTd", [D, S], BF16, "Internal")
    QTd = dr("QTd", [D, S], BF16, "Internal")
    Od = dr("Od", [S, D], BF16, "Internal")

    ISQ = float(1.0 / np.sqrt(D))

    _uid = [0]

    def un(n):
        _uid[0] += 1
        return f"{n}_{_uid[0]}"

    with ExitStack() as ges:
        sc = Sched(nc, ges)
        add = sc.add
        gsb = lambda n, s, d: ges.enter_context(nc.sbuf_tensor(n, s, d))
        ident = gsb("ident", [128, 128], BF16)
        maskT = gsb("maskT", [128, 128], BF16)
        maskQ = gsb("maskQ", [128, 128], BF16)
        maskfirst = gsb("maskfirst", [128, 1024], BF16)
        zeros = gsb("zeros", [128, 1024], BF16)
        epst = gsb("epst", [128, 1], F32)
        stats = gsb("stats", [128, 3 * 8], F32)
        STAT = Ring([stats[:, 3 * i:3 * i + 3] for i in range(8)])
        s_const = sc.newsem("s_const")

        t_c = add("pool", lambda e: e.dma_start(out=ident[:], in_=c_ident), sem=s_const, dma=True)
        t_c = add("pool", lambda e: e.dma_start(out=maskT[:], in_=c_maskT), sem=s_const, dma=True)
        t_c = add("pool", lambda e: e.dma_start(out=maskQ[:], in_=c_maskQ), sem=s_const, dma=True)
        t_c = add("pool", lambda e: e.dma_start(out=maskfirst[:], in_=c_maskfirst), sem=s_const, dma=True)
        onesb = gsb("onesb", [128, 128], BF16)
        add("dve", lambda e: e.memset(zeros[:], 0.0))
        add("dve", lambda e: e.memset(onesb[:], 1.0))
        add("dve", lambda e: e.memset(epst[:], EPS))
        sc.barrier()

        def load_gain(tile, row, sem):
            return add("sp", lambda e: e.dma_start(out=tile[:], in_=gains[row:row + 1, :].partition_broadcast(128)),
                       sem=sem, dma=True)

        def rstd_chain(src_ap, t_src, extra_waits=()):
            si, st, sw = STAT.acquire()
            t_ms = add("act", lambda e: e.activation(out=junk[:], in_=src_ap, func=AF.Square, scale=ISQ,
                                                     accum_out=st[:, 0:1]), waits=[t_src] + sw + list(extra_waits))
            t_sd = add("act", lambda e: e.activation(out=st[:, 1:2], in_=st[:, 0:1], func=AF.Sqrt, bias=epst[:],
                                                     scale=1.0), waits=[t_ms])
            t_rs = add("dve", lambda e: e.reciprocal(out=st[:, 2:3], in_=st[:, 1:2]), waits=[t_sd])
            return si, st[:, 2:3], t_rs

        def transpose8(src_tile, t_src, TP, dst_view, dst_waits):
            pi, tp, pw_ = TP.acquire()

            def f(e):
                for kc in range(KC):
                    ins = e.transpose(out=tp[:, kc, :], in_=src_tile[:, kc * 128:(kc + 1) * 128], identity=ident[:])
                return ins
            t_T = add("pe", f, waits=[t_src] + pw_)
            t_ev = add("act", lambda e: e.activation(out=dst_view, in_=tp[:], func=AF.Copy), waits=[t_T] + list(dst_waits))
            TP.release(pi, t_ev)
            return t_T, t_ev

        def norm_T(src_rows, gtile, t_g, XT, HB, TP, dst_view, dst_waits, keep_x=False):
            xi, xt, xw = XT.acquire()
            t_ld = add("sp", lambda e: e.dma_start(out=xt[:], in_=src_rows), waits=xw, sem=XT.sems[xi], dma=True)
            si, rs, t_rs = rstd_chain(xt[:], t_ld)
            hi, hb, hw = HB.acquire()
            t_h = add("dve", lambda e: e.scalar_tensor_tensor(out=hb[:], in0=xt[:], scalar=rs, in1=gtile[:],
                                                              op0=ALU.mult, op1=ALU.mult), waits=[t_rs, t_ld, t_g] + hw)
            STAT.release(si, t_h)
            t_T, t_ev = transpose8(hb, t_h, TP, dst_view, dst_waits)
            HB.release(hi, t_T)
            if not keep_x:
                XT.release(xi, t_h)
            return xi, xt, t_ld, t_h, t_ev, hb, hi

        def norm_part(src_rows, gtile, t_g, XT, HB):
            xi, xt, xw = XT.acquire()
            t_ld = add("sp", lambda e: e.dma_start(out=xt[:], in_=src_rows), waits=xw, sem=XT.sems[xi], dma=True)
            si, rs, t_rs = rstd_chain(xt[:], t_ld)
            hi, hb, hw = HB.acquire()
            t_h = add("dve", lambda e: e.scalar_tensor_tensor(out=hb[:], in0=xt[:], scalar=rs, in1=gtile[:],
                                                              op0=ALU.mult, op1=ALU.mult), waits=[t_rs, t_ld, t_g] + hw)
            STAT.release(si, t_h)
            return dict(xi=xi, xt=xt, t_ld=t_ld, hi=hi, hb=hb, t_h=t_h)

        def T_part(nd, TP, HB, dst_view, dst_waits):
            t_T, t_ev = transpose8(nd["hb"], nd["t_h"], TP, dst_view, dst_waits)
            HB.release(nd["hi"], t_T)
            return t_ev

        def epilogue(y_ap, t_y, gtile, t_g, xr, t_xr, T2, dst_rows):
            si, rs, t_rs = rstd_chain(y_ap, t_y)
            ti, t2, tw = T2.acquire()
            t_t2 = add("dve", lambda e: e.scalar_tensor_tensor(out=t2[:], in0=y_ap, scalar=rs, in1=gtile[:],
                                                               op0=ALU.mult, op1=ALU.mult), waits=[t_rs, t_y, t_g] + tw)
            STAT.release(si, t_t2)
            t_add = add("pool", lambda e: e.tensor_tensor(out=t2[:], in0=t2[:], in1=xr[:], op=ALU.add), waits=[t_t2, t_xr])
            t_st = add("pool", lambda e: e.dma_start(out=dst_rows, in_=t2[:]), waits=[t_add], sem=T2.sems[ti], dma=True)
            T2.release(ti, t_st)
            return t_t2, t_st, t_add

        def mkring(es, name, n, shape, dtype, sems=False, psum=False):
            alloc = nc.psum_tensor if psum else nc.sbuf_tensor
            r = Ring([es.enter_context(alloc(un(f"{name}{i}"), shape, dtype)) for i in range(n)])
            if sems:
                r.sems = [namedsem(f"s_{name}{i}") for i in range(n)]
            return r

        _semcache = {}

        def namedsem(name):
            if name not in _semcache:
                _semcache[name] = sc.newsem(name)
            return _semcache[name]

        def phase_pool(l, src, dst, Wts=None):
            nonlocal junk
            with ExitStack() as es:
                sb = lambda n, s, d: es.enter_context(nc.sbuf_tensor(un(n), s, d))
                junk = sb("junk", [128, D], BF16)
                pw = sb("pw", [128, 4, 2, 256], BF16)
                PB = sb("PB", [128, 12, 128], BF16)
                gpre = sb("gpre", [128, D], F32)
                gpost = sb("gpost", [128, D], F32)
                psc = sb("psc", [128, D], F32)
                s_w = namedsem("s_w")
                for g in range(4):
                    t_w = add("pool", lambda e, g=g: e.dma_start(out=pw[:, g, :, :], in_=pool_w[l, g].rearrange("(j p) d -> p j d", p=128)),
                              sem=s_w, dma=True)
                t_w = add("pool", lambda e: e.dma_start(out=PB[:], in_=c_poolB.rearrange("m p a -> p m a")), sem=s_w, dma=True)
                t_w = load_gain(gpre, 0 + l, s_w)
                t_w = load_gain(gpost, 4 + l, s_w)
                t_w = load_gain(psc, 17 + l, s_w)
                XT = mkring(es, "xt", 2, [128, D], F32, sems=True)
                XR = mkring(es, "xr", 2, [128, D], F32, sems=True)
                HB = mkring(es, "hb", 3, [128, D], BF16)
                T1 = mkring(es, "t1", 2, [128, D], F32)
                T2 = mkring(es, "t2", 2, [128, D], F32, sems=True)
                YB = mkring(es, "yb", 2, [128, KC, 128], BF16)
                YP = mkring(es, "yp", 2, [128, KC, 128], F32, psum=True)
                MP = mkring(es, "mp", 2, [128, D], F32, psum=True)
                st = [dict() for _ in range(NT)]

                def A(it):
                    i = NT - 1 - it
                    d = st[it]
                    d["rows"] = slice(i * 128, (i + 1) * 128)
                    d.update(norm_part(src[d["rows"], :], gpre, t_w, XT, HB))
                    XT.release(d["xi"], d["t_h"])

                def B1(it):
                    d = st[it]
                    prev = st[it - 1] if it > 0 else None
                    hb = d["hb"]
                    yi, yp, yw = YP.acquire()

                    def f_mix(e):
                        for cc in range(KC):
                            g = cc // 2
                            if prev is None:
                                ins = e.matmul(yp[:, cc, :], lhsT=hb[:, cc * 128:(cc + 1) * 128], rhs=PB[:, 8 + g, :], start=True, stop=True)
                            else:
                                e.matmul(yp[:, cc, :], lhsT=hb[:, cc * 128:(cc + 1) * 128], rhs=PB[:, g, :], start=True, stop=False)
                                ins = e.matmul(yp[:, cc, :], lhsT=prev["hb"][:, cc * 128:(cc + 1) * 128], rhs=PB[:, 4 + g, :], start=False, stop=True)
                        return ins
                    t_mix = add("pe", f_mix, waits=[d["t_h"], t_w] + yw + ([prev["t_h"]] if prev else []))
                    if prev is not None:
                        HB.release(prev["hi"], t_mix)
                    if it == NT - 1:
                        HB.release(d["hi"], t_mix)
                    bi, yb, bw = YB.acquire()
                    t_yb = add("act", lambda e: e.activation(out=yb[:], in_=yp[:], func=AF.Copy), waits=[t_mix] + bw)
                    YP.release(yi, t_yb)
                    d.update(bi=bi, yb=yb, t_yb=t_yb)

                def B2(it):
                    d = st[it]
                    yb = d["yb"]
                    mi, mp, mw = MP.acquire()

                    def f_lin(e):
                        for g in range(4):
                            for j in range(2):
                                ins = e.matmul(mp[:, g * 256:(g + 1) * 256], lhsT=yb[:, 2 * g + j, :], rhs=pw[:, g, j, :],
                                               start=(j == 0), stop=(j == 1))
                        return ins
                    t_lin = add("pe", f_lin, waits=[d["t_yb"]] + mw)
                    YB.release(d["bi"], t_lin)
                    ui, t1, uw = T1.acquire()
                    t_t1 = add("dve", lambda e: e.tensor_tensor(out=t1[:], in0=mp[:], in1=psc[:], op=ALU.mult), waits=[t_lin] + uw)
                    MP.release(mi, t_t1)
                    d.update(ui=ui, t1=t1, t_t1=t_t1)
                    ri, xr, rw = XR.acquire()
                    rows = d["rows"]
                    t_xr = add("sp", lambda e: e.dma_start(out=xr[:], in_=src[rows, :]), waits=rw, sem=XR.sems[ri], dma=True)
                    d.update(ri=ri, xr=xr, t_xr=t_xr)

                def C(it):
                    d = st[it]
                    t_t2, t_st, t_add = epilogue(d["t1"][:], d["t_t1"], gpost, t_w, d["xr"], d["t_xr"], T2, dst[d["rows"], :])
                    T1.release(d["ui"], t_t2)
                    XR.release(d["ri"], t_add)
                    if Wts is not None and Wts["thunks"]:
                        Wts["thunks"].pop(0)()

                for n in range(NT + 3):
                    if n < NT:
                        A(n)
                    if 0 <= n - 1 < NT:
                        B1(n - 1)
                    if 0 <= n - 2 < NT:
                        B2(n - 2)
                    if 0 <= n - 3 < NT:
                        C(n - 3)
                sc.barrier()
                sc.emit()

        def load_w_rows(dst3, src2, nk, width, sem, chunk):
            srcv = src2.rearrange("(k p) f -> p k f", p=128)
            t = None
            for k0 in range(0, nk, chunk):
                t = add("pool", lambda e, k0=k0: e.dma_start(out=dst3[:, k0:k0 + chunk, :], in_=srcv[:, k0:k0 + chunk, :]),
                        sem=sem, dma=True)
            return t

        def mlp_weights(wes, l, part="all", Wts=None):
            sbw = lambda n, s, d: wes.enter_context(nc.sbuf_tensor(un(n), s, d))
            if Wts is None:
                Wts = dict(thunks=[], sem=namedsem("s_wmlp"))
            s_w = Wts["sem"]
            thunks = Wts["thunks"]
            if part in ("up", "all"):
                wup = sbw("wup", [128, KC, FF], BF16)
                Wts["wup"] = wup
                upv = w_up[l].rearrange("(k p) f -> p k f", p=128)
                for k0 in range(KC):
                    thunks.append(lambda k0=k0: add("pool", lambda e: e.dma_start(out=wup[:, k0:k0 + 1, :], in_=upv[:, k0:k0 + 1, :]),
                                                    sem=s_w, dma=True))
            if part in ("down", "all"):
                wdn = sbw("wdn", [128, FC, D], BF16)
                gpre = sbw("gpre", [128, D], F32)
                gpost = sbw("gpost", [128, D], F32)
                Wts.update(wdn=wdn, gpre=gpre, gpost=gpost)
                dnv = w_down[l].rearrange("(k p) f -> p k f", p=128)
                for k0 in range(0, FC, 4):
                    thunks.append(lambda k0=k0: add("pool", lambda e: e.dma_start(out=wdn[:, k0:k0 + 4, :], in_=dnv[:, k0:k0 + 4, :]),
                                                    sem=s_w, dma=True))
                thunks.append(lambda: load_gain(gpre, 8 + l, s_w))
                thunks.append(lambda: load_gain(gpost, 12 + l, s_w))
            return Wts

        def flush_w(Wts):
            while Wts["thunks"]:
                Wts["thunks"].pop(0)()

        def phase_mlp(l, src, dst, Wts):
            nonlocal junk
            TT = 256
            NS = TT // 128
            NTT = S // TT
            with ExitStack() as es:
                sb = lambda n, s, d: es.enter_context(nc.sbuf_tensor(un(n), s, d))
                junk = sb("junk", [128, D], BF16)
                flush_w(Wts)
                wup, wdn, gpre, gpost = Wts["wup"], Wts["wdn"], Wts["gpre"], Wts["gpost"]
                t_w = (Wts["sem"], Wts["sem"].n)
                XT = mkring(es, "xt", 6, [128, D], F32, sems=True)
                HB = mkring(es, "hb", 2, [128, D], BF16)
                T2 = mkring(es, "t2", 2, [128, D], F32, sems=True)
                R = mkring(es, "r", 3, [128, TT], BF16)
                hT = sb("hT", [128, KC, TT], BF16)
                uT = sb("uT", [128, FC, TT], BF16)
                TP = mkring(es, "tp", 2, [128, KC, 128], BF16, psum=True)
                UP = mkring(es, "up", 2, [128, 512], F32, psum=True)
                DN = mkring(es, "dn", 2, [128, D], F32, psum=True)
                xinfo = {}
                hT_rd = []
                uT_rd = []

                nds = {}

                def do_norm(t):
                    for s in range(NS):
                        rows = slice(t * TT + s * 128, t * TT + (s + 1) * 128)
                        nd = norm_part(src[rows, :], gpre, t_w, XT, HB)
                        nds[(t, s)] = nd
                        xinfo[(t, s)] = (nd["xi"], nd["xt"], nd["t_ld"])

                def do_T(t):
                    toks = []
                    for s in range(NS):
                        toks.append(T_part(nds.pop((t, s)), TP, HB, hT[:, :, s * 128:(s + 1) * 128], hT_rd))
                    return toks

                def do_up(t, t_hT):
                    last = None
                    tks = []
                    for fc in range(FC):
                        ui, up, uw = UP.acquire()

                        def f(e, fc=fc, up=up):
                            for kc in range(KC):
                                ins = e.matmul(up[:, 0:TT], lhsT=wup[:, kc, fc * 128:(fc + 1) * 128], rhs=hT[:, kc, :],
                                               start=(kc == 0), stop=(kc == KC - 1))
                            return ins
                        t_mm = add("pe", f, waits=list(t_hT) + [t_w] + uw)
                        ri, r, rw = R.acquire()
                        t_r = add("act", lambda e, r=r, up=up: e.activation(out=r[:], in_=up[:, 0:TT], func=AF.Relu), waits=[t_mm] + rw)
                        UP.release(ui, t_r)
                        t_u = add("dve", lambda e, fc=fc, r=r: e.tensor_tensor(out=uT[:, fc, :], in0=r[:], in1=r[:], op=ALU.mult),
                                  waits=[t_r] + (uT_rd if fc == 0 else []))
                        R.release(ri, t_u)
                        tks.append(t_u)
                        last = t_mm
                    return last, tks

                def do_down(t, t_us):
                    last = None
                    for s in range(NS):
                        rows = slice(t * TT + s * 128, t * TT + (s + 1) * 128)
                        di, dn, dw = DN.acquire()

                        def f(e, s=s, dn=dn):
                            for dh in range(2):
                                for fc in range(FC):
                                    ins = e.matmul(dn[:, dh * 512:(dh + 1) * 512], lhsT=uT[:, fc, s * 128:(s + 1) * 128],
                                                   rhs=wdn[:, fc, dh * 512:(dh + 1) * 512], start=(fc == 0), stop=(fc == FC - 1))
                            return ins
                        t_mm = add("pe", f, waits=[t_us[-1]] + dw)
                        xi, xt, t_ld = xinfo.pop((t, s))
                        t_t2, t_st, t_add = epilogue(dn[:], t_mm, gpost, t_w, xt, t_ld, T2, dst[rows, :])
                        DN.release(di, t_t2)
                        XT.release(xi, t_add)
                        last = t_mm
                    return last

                do_norm(0)
                t_hT = do_T(0)
                for t in range(NTT):
                    if t + 1 < NTT:
                        do_norm(t + 1)
                    t_upmm, t_us = do_up(t, t_hT)
                    hT_rd[:] = [t_upmm]
                    if t + 1 < NTT:
                        t_hT = do_T(t + 1)
                    t_dn = do_down(t, t_us)
                    uT_rd[:] = [t_dn]
                sc.barrier()
                sc.emit()

        def phase_proj(kind, j, src):
            nonlocal junk
            TT = 512
            NS = 4
            NTT = S // TT
            with ExitStack() as es:
                sb = lambda n, s, d: es.enter_context(nc.sbuf_tensor(un(n), s, d))
                junk = sb("junk", [128, D], BF16)
                s_w = namedsem("s_w")
                gpre = sb("gpre", [128, D], F32)
                if kind == "kv":
                    W = sb("wkv", [128, KC, 2 * D], BF16)
                    load_w_rows(W, w_kv, KC, 2 * D, s_w, 1)
                    t_w = load_gain(gpre, 16, s_w)
                    dstT = KTd
                else:
                    W = sb("wq", [128, KC, D], BF16)
                    load_w_rows(W, w_q[j], KC, D, s_w, 2)
                    t_w = load_gain(gpre, 2 + j, s_w)
                    dstT = QTd
                dstTv = dstT.rearrange("(h p) s -> p h s", p=128)
                XT = mkring(es, "xt", 3, [128, D], F32, sems=True)
                HB = mkring(es, "hb", 3, [128, D], BF16)
                HT = mkring(es, "hT", 2, [128, KC, TT], BF16)
                KS = mkring(es, "ks", 2, [128, KC, TT], BF16, sems=True)
                TP = mkring(es, "tp", 2, [128, KC, 128], BF16, psum=True)
                KP = mkring(es, "kp", 2, [128, 512], F32, psum=True)
                if kind == "kv":
                    VP = mkring(es, "vp", 2, [128, D], F32, psum=True)
                    VS = mkring(es, "vs", 3, [128, D], F32, sems=True)
                    VB = mkring(es, "vb", 2, [128, D], F32, sems=True)
                    DV = mkring(es, "dv", 2, [128, D], BF16, sems=True)
                    s_z = namedsem("s_vz")
                    add("sp", lambda e: e.dma_start(out=Vd[S:S + 1, 0:512], in_=zeros[0:1, :].bitcast(F32)), sem=s_z, dma=True)
                    vstate = dict(t_prev_vst=add("sp", lambda e: e.dma_start(out=Vd[S:S + 1, 512:1024], in_=zeros[0:1, :].bitcast(F32)), sem=s_z, dma=True))
                order = [(t, s_) for t in reversed(range(NTT)) for s_ in range(NS)]
                tiles = {}
                nds = {}

                def A(n):
                    t, s_ = order[n]
                    rows = slice(t * TT + s_ * 128, t * TT + (s_ + 1) * 128)
                    nd = norm_part(src[rows, :], gpre, t_w, XT, HB)
                    XT.release(nd["xi"], nd["t_h"])
                    nds[n] = nd

                def B(n):
                    t, s_ = order[n]
                    if s_ == 0:
                        hi_, hT, hw_ = HT.acquire()
                        tiles[t] = dict(hi=hi_, hT=hT, hw=hw_, toks=[], rd=[])
                    td = tiles[t]
                    t_ev = T_part(nds.pop(n), TP, HB, td["hT"][:, :, s_ * 128:(s_ + 1) * 128], td["hw"])
                    td["toks"].append(t_ev)

                def Cpart(t, q):
                    td = tiles[t]
                    hT, toks = td["hT"], td["toks"]
                    if q == 0:
                        ki, ks, kw = KS.acquire()
                        td.update(ki=ki, ks=ks, kw=kw)
                    ks = td["ks"]
                    for hp in (2 * q, 2 * q + 1):
                        pi, kp, pw_ = KP.acquire()

                        def f(e, hp=hp, kp=kp):
                            for kc in range(KC):
                                ins = e.matmul(kp[:], lhsT=W[:, kc, hp * 128:(hp + 1) * 128], rhs=hT[:, kc, :],
                                               start=(kc == 0), stop=(kc == KC - 1))
                            return ins
                        t_mm = add("pe", f, waits=toks + [t_w] + pw_)
                        t_kev = add("act", lambda e, hp=hp, kp=kp: e.activation(out=ks[:, hp, :], in_=kp[:], func=AF.Copy),
                                    waits=[t_mm] + (td["kw"] if hp == 0 else []))
                        KP.release(pi, t_kev)
                        td["rd"] = [t_mm]
                    if q == 3:
                        t_kst = add("pool", lambda e: e.dma_start(out=dstTv[:, :, t * TT:(t + 1) * TT], in_=ks[:]),
                                    waits=[t_kev], sem=KS.sems[td["ki"]], dma=True)
                        KS.release(td["ki"], t_kst)
                    if kind == "kv":
                        s_ = NS - 1 - q
                        p0 = t * TT + s_ * 128
                        vi, vp, vw = VP.acquire()

                        def f(e):
                            for dh in range(2):
                                for kc in range(KC):
                                    ins = e.matmul(vp[:, dh * 512:(dh + 1) * 512], lhsT=hT[:, kc, s_ * 128:(s_ + 1) * 128],
                                                   rhs=W[:, kc, D + dh * 512:D + (dh + 1) * 512], start=(kc == 0), stop=(kc == KC - 1))
                            return ins
                        t_mm = add("pe", f, waits=toks + [t_w] + vw)
                        td["rd"] = [t_mm]
                        si_, vs, sw_ = VS.acquire()
                        t_vev = add("act", lambda e: e.activation(out=vs[:], in_=vp[:], func=AF.Copy), waits=[t_mm] + sw_)
                        VP.release(vi, t_vev)
                        t_vst = add("pool", lambda e: e.dma_start(out=Vd[p0:p0 + 128, :], in_=vs[:]), waits=[t_vev],
                                    sem=VS.sems[si_], dma=True)
                        bi, vb, bw = VB.acquire()
                        t_vb = add("pool", lambda e: e.dma_start(out=vb[:], in_=Vd[p0 + 1:p0 + 129, :]),
                                   waits=[t_vst, vstate["t_prev_vst"]] + bw, sem=VB.sems[bi], dma=True)
                        vstate["t_prev_vst"] = t_vst
                        di, dv, dw = DV.acquire()
                        t_dv = add("dve", lambda e: e.tensor_tensor(out=dv[:], in0=vb[:], in1=vs[:], op=ALU.subtract),
                                   waits=[t_vb, t_vev] + dw)
                        VB.release(bi, t_dv)
                        VS.release(si_, t_dv, t_vst)
                        t_dst = add("pool", lambda e: e.dma_start(out=dVd[p0:p0 + 128, :], in_=dv[:]), waits=[t_dv],
                                    sem=DV.sems[di], dma=True)
                        DV.release(di, t_dst)
                    if q == 3:
                        HT.release(td["hi"], *td["rd"])

                NSUB = len(order)
                for n in range(NSUB + 1 + NS):
                    if n < NSUB:
                        A(n)
                    if 0 <= n - 1 < NSUB:
                        B(n - 1)
                    m = n - 1 - NS
                    if m >= 0 and m < NSUB:
                        tprev, q = order[m][0], m % NS
                        Cpart(tprev, q)
                sc.barrier()
                sc.emit()

        def phase_attn(Wts=None):
            with ExitStack() as es:
                KT = mkring(es, "KT", 2, [128, S], BF16, sems=True)
                QT = mkring(es, "QT", 2, [128, S], BF16, sems=True)
                DVt = mkring(es, "DVt", 2, [128, NT, 128], BF16, sems=True)
                VSH = mkring(es, "VSH", 2, [128, NT, 128], F32, sems=True)
                OSB = mkring(es, "OSB", 2, [128, NT, 128], BF16, sems=True)
                CW = 1024
                OM = mkring(es, "om", 3, [128, CW], F32)
                PR = mkring(es, "pr", 4, [128, CW], BF16)
                PTS = mkring(es, "pts", 3, [128, 8, 128], BF16)
                OTMP = mkring(es, "otmp", 2, [128, 128], F32)
                ZP = mkring(es, "zp", 2, [128, CW], F32, psum=True)
                PTP = mkring(es, "ptp", 2, [128, 8, 128], BF16, psum=True)
                OP = mkring(es, "op", 2, [128, 512], F32, psum=True)
                dVv = dVd.rearrange("(n p) d -> p n d", p=128)
                Vshv = Vd[1:S + 1, :].rearrange("(n p) d -> p n d", p=128)
                Odv = Od.rearrange("(n p) d -> p n d", p=128)
                items = []
                pairs = []

                def issue_pair_loads(hp):
                    pair = pairs[hp]
                    cols = pair["cols"]
                    ki, kt, kw = KT.acquire()
                    qi, qt, qw = QT.acquire()
                    di, dvt, dw = DVt.acquire()
                    vi, vsh, vw = VSH.acquire()
                    oi, osb, ow = OSB.acquire()
                    t_k = add("sp", lambda e: e.dma_start(out=kt[:], in_=KTd[cols, :]), waits=kw, sem=KT.sems[ki], dma=True)
                    t_q = add("sp", lambda e: e.dma_start(out=qt[:], in_=QTd[cols, :]), waits=qw, sem=QT.sems[qi], dma=True)
                    t_d = add("sp", lambda e: e.dma_start(out=dvt[:], in_=dVv[:, :, cols]), waits=dw, sem=DVt.sems[di], dma=True)
                    t_v = add("sp", lambda e: e.dma_start(out=vsh[:], in_=Vshv[:, :, cols]), waits=vw, sem=VSH.sems[vi], dma=True)
                    pair.update(kt=kt, qt=qt, dvt=dvt, vsh=vsh, osb=osb, t_k=t_k, t_q=t_q, t_d=t_d, t_v=t_v, ow=ow,
                                ki=ki, qi=qi, di=di, vi=vi, oi=oi)

                NPAIR = KC
                NQ = NT
                for hp in range(NPAIR):
                    pair = dict(hp=hp, cols=slice(hp * 128, (hp + 1) * 128))
                    pairs.append(pair)
                    for i in range(NT - NQ, NT):
                        grp = dict(pair=pair, i=i, n_left=0, first=True)
                        nch = (S - 128 * i + CW - 1) // CW
                        for c in range(nch):
                            k0 = 128 * i + CW * c
                            w = min(CW, S - k0)
                            for hd in range(2):
                                items.append(dict(pair=pair, grp=grp, i=i, c=c, k0=k0, w=w, hd=hd, last=(c == nch - 1)))
                                grp["n_left"] += 1
                issue_pair_loads(0)
                if NPAIR > 1:
                    issue_pair_loads(1)
                N = len(items)
                o_state = {}

                def st_QK(k):
                    it = items[k]
                    zi, zp, zw = ZP.acquire()
                    it["zi"], it["zp"] = zi, zp
                    hs = slice(it["hd"] * 64, it["hd"] * 64 + 64)
                    pr_, i, k0, w = it["pair"], it["i"], it["k0"], it["w"]
                    def f(e):
                        for j0 in range(0, w, 512):
                            wj = min(512, w - j0)
                            ins = e.matmul(zp[:, j0:j0 + wj], lhsT=pr_["qt"][hs, i * 128:(i + 1) * 128],
                                           rhs=pr_["kt"][hs, k0 + j0:k0 + j0 + wj], start=True, stop=True)
                        return ins
                    it["t_qk"] = add("pe", f, waits=[pr_["t_k"], pr_["t_q"]] + zw)

                def st_sig(k):
                    it = items[k]
                    mi, om, mw = OM.acquire()
                    it["mi"], it["om"] = mi, om
                    zp, w = it["zp"], it["w"]
                    def f(e):
                        for j0 in range(0, w, 512):
                            wj = min(512, w - j0)
                            ins = e.activation(out=om[:, j0:j0 + wj], in_=zp[:, j0:j0 + wj], func=AF.Sigmoid, scale=-0.125)
                        return ins
                    it["t_sig"] = add("act", f, waits=[it["t_qk"]] + mw)
                    ZP.release(it["zi"], it["t_sig"])

                def st_scan(k):
                    it = items[k]
                    pi, p, pw_ = PR.acquire()
                    it["pi"], it["p"] = pi, p
                    om, w = it["om"], it["w"]
                    if it["c"] == 0:
                        d1 = maskfirst[:, 0:w]
                        init = 1.0
                        cw = []
                    else:
                        prev = items[k - 2]
                        d1 = zeros[:, 0:w]
                        init = prev["p"][:, prev["w"] - 1:prev["w"]]
                        cw = [prev["t_scan"]]
                    it["t_scan"] = add("dve", lambda e: e.tensor_tensor_scan(out=p[:, 0:w], data0=om[:, 0:w], data1=d1, initial=init,
                                                                            op0=ALU.mult, op1=ALU.max),
                                       waits=[it["t_sig"]] + pw_ + cw)
                    OM.release(it["mi"], it["t_scan"])
                    it["t_p"] = it["t_scan"]
                    if it["c"] == 0:
                        it["t_p"] = add("pool", lambda e: e.tensor_tensor(out=p[:, 0:128], in0=p[:, 0:128], in1=maskQ[:], op=ALU.mult),
                                        waits=[it["t_scan"]])

                def st_T(k):
                    it = items[k]
                    ti, tp, tw = PTP.acquire()
                    it["ti"], it["tp"] = ti, tp
                    p, nsub = it["p"], it["w"] // 128

                    def f(e):
                        for sbk in range(nsub):
                            ins = e.transpose(out=tp[:, sbk, :], in_=p[:, sbk * 128:(sbk + 1) * 128], identity=ident[:])
                        return ins
                    it["t_T"] = add("pe", f, waits=[it["t_p"]] + tw)
                    PR.release(it["pi"], it["t_T"])

                def st_evac(k):
                    it = items[k]
                    si_, pts, sw_ = PTS.acquire()
                    it["si"], it["pts"] = si_, pts
                    tp, nsub = it["tp"], it["w"] // 128
                    if True:
                        t_e = add("act", lambda e: e.activation(out=pts[:, 0:nsub, :], in_=tp[:, 0:nsub, :], func=AF.Copy),
                                  waits=[it["t_T"]] + sw_)
                    else:
                        t_e = add("dve", lambda e: e.tensor_copy(out=pts[:, 0:nsub, :], in_=tp[:, 0:nsub, :]),
                                  waits=[it["t_T"]] + sw_)
                    toks = [t_e]
                    it["t_ev"] = toks
                    PTP.release(it["ti"], *toks)

                def st_AV(k):
                    it = items[k]
                    grp, pr_ = it["grp"], it["pair"]
                    gw = []
                    if grp["first"]:
                        grp["first"] = False
                        oi, op, ow = OP.acquire()
                        grp["oi"], grp["op"] = oi, op
                        gw = ow
                    op = grp["op"]
                    pts, nsub, hd, k0 = it["pts"], it["w"] // 128, it["hd"], it["k0"]
                    oc = slice(hd * 64, hd * 64 + 64)
                    c, last = it["c"], it["last"]

                    def f(e):
                        for sbk in range(nsub):
                            ins = e.matmul(op[:, oc], lhsT=pts[:, sbk, :], rhs=pr_["dvt"][:, k0 // 128 + sbk, oc],
                                           start=(c == 0 and sbk == 0 and hd == 0), stop=(last and sbk == nsub - 1 and hd == 1),
                                           skip_group_check=True)
                        return ins
                    t_av = add("pe", f, waits=it["t_ev"] + [pr_["t_d"]] + gw)
                    PTS.release(it["si"], t_av)
                    grp["n_left"] -= 1
                    if grp["n_left"] == 0:
                        i = it["i"]
                        osb, vsh = pr_["osb"], pr_["vsh"]
                        mi_, otmp, mw_ = OTMP.acquire()
                        t_oa = add("act", lambda e: e.activation(out=otmp[:], in_=op[:, 0:128], func=AF.Copy), waits=[t_av] + mw_)
                        OP.release(grp["oi"], t_oa)
                        t_o = add("pool", lambda e: e.tensor_tensor(out=osb[:, i, :], in0=otmp[:], in1=vsh[:, i, :], op=ALU.add),
                                  waits=[t_oa, pr_["t_v"]] + pr_["ow"])
                        OTMP.release(mi_, t_o)
                        if i == NT - 1:
                            cols = pr_["cols"]
                            t_st = add("sp", lambda e: e.dma_start(out=Odv[:, :, cols], in_=osb[:]), waits=[t_o],
                                       sem=OSB.sems[pr_["oi"]], dma=True)
                            OSB.release(pr_["oi"], t_st)
                            KT.release(pr_["ki"], t_av)
                            QT.release(pr_["qi"], t_av)
                            DVt.release(pr_["di"], t_av)
                            VSH.release(pr_["vi"], t_o)
                            if pr_["hp"] + 2 < NPAIR:
                                issue_pair_loads(pr_["hp"] + 2)
                            if Wts is not None and Wts["thunks"]:
                                Wts["thunks"].pop(0)()

                for n in range(N + 5):
                    if n < N:
                        st_QK(n)
                        st_sig(n)
                        st_scan(n)
                    if 0 <= n - 3 < N:
                        st_T(n - 3)
                        st_evac(n - 3)
                    if 0 <= n - 5 < N:
                        st_AV(n - 5)
                sc.barrier()
                sc.emit()

        def phase_oproj(j, src, dst, Wts=None):
            nonlocal junk
            with ExitStack() as es:
                sb = lambda n, s, d: es.enter_context(nc.sbuf_tensor(un(n), s, d))
                junk = sb("junk", [128, D], BF16)
                s_w = namedsem("s_w")
                wo = sb("wo", [128, KC, D], BF16)
                gpost = sb("gpost", [128, D], F32)
                load_w_rows(wo, w_o[j], KC, D, s_w, 2)
                t_w = load_gain(gpost, 4 + 2 + j, s_w)
                OB = mkring(es, "ob", 3, [128, D], BF16, sems=True)
                XR = mkring(es, "xr", 4, [128, D], F32, sems=True)
                OT = mkring(es, "ot", 3, [128, KC, 128], BF16)
                T2 = mkring(es, "t2", 2, [128, D], F32, sems=True)
                TP = mkring(es, "tp", 2, [128, KC, 128], BF16, psum=True)
                YP = mkring(es, "yp", 2, [128, D], F32, psum=True)
                st = [dict() for _ in range(NT)]

                def A(i):
                    d = st[i]
                    rows = slice(i * 128, (i + 1) * 128)
                    bi, ob, bw = OB.acquire()
                    t_ob = add("sp", lambda e: e.dma_start(out=ob[:], in_=Od[rows, :]), waits=bw, sem=OB.sems[bi], dma=True)
                    xi, xr, xw = XR.acquire()
                    t_xr = add("sp", lambda e: e.dma_start(out=xr[:], in_=src[rows, :]), waits=xw, sem=XR.sems[xi], dma=True)
                    ti, ot, tw = OT.acquire()
                    t_T, t_ev = transpose8(ob, t_ob, TP, ot[:], tw)
                    OB.release(bi, t_T)
                    d.update(rows=rows, xi=xi, xr=xr, t_xr=t_xr, ti=ti, ot=ot, t_ev=t_ev)

                def B(i):
                    d = st[i]
                    ot = d["ot"]
                    yi, yp, yw = YP.acquire()

                    def f(e):
                        for dh in range(2):
                            for kc in range(KC):
                                ins = e.matmul(yp[:, dh * 512:(dh + 1) * 512], lhsT=ot[:, kc, :], rhs=wo[:, kc, dh * 512:(dh + 1) * 512],
                                               start=(kc == 0), stop=(kc == KC - 1))
                        return ins
                    t_mm = add("pe", f, waits=[d["t_ev"], t_w] + yw)
                    OT.release(d["ti"], t_mm)
                    t_t2, t_st, t_add = epilogue(yp[:], t_mm, gpost, t_w, d["xr"], d["t_xr"], T2, dst[d["rows"], :])
                    YP.release(yi, t_t2)
                    XR.release(d["xi"], t_add)
                    if Wts is not None and Wts["thunks"]:
                        Wts["thunks"].pop(0)()

                for n in range(NT + 1):
                    if n < NT:
                        A(n)
                    if 0 <= n - 1 < NT:
                        B(n - 1)
                sc.barrier()
                sc.emit()

        junk = None
        sc.emit()
        def pool_mlp(l, src):
            with ExitStack() as wes:
                Wts = mlp_weights(wes, l)
                phase_pool(l, src, xs, Wts)
                phase_mlp(l, xs, xs, Wts)

        def attn_oproj_mlp(j):
            with ExitStack() as wes:
                Wts = mlp_weights(wes, 2 + j, "up")
                phase_attn(Wts)
                flush_w(Wts)
                with ExitStack() as wes2:
                    mlp_weights(wes2, 2 + j, "down", Wts)
                    phase_oproj(j, xs, xs, Wts)
                    phase_mlp(2 + j, xs, out if j == 1 else xs, Wts)

        plist = [lambda: pool_mlp(0, x_in), lambda: pool_mlp(1, xs), lambda: phase_proj("kv", 0, xs)]
        for j in range(2):
            plist += [lambda j=j: phase_proj("q", j, xs), lambda j=j: attn_oproj_mlp(j)]
        for ph in plist:
            ph()
        add("sp", lambda e: e.nop())
        add("act", lambda e: e.nop())
        sc.emit()
    return nc


def host_consts():
    ident = np.eye(128, dtype=np.float32)
    a = np.arange(128)
    maskfirst = np.zeros((128, 1024), np.float32)
    maskfirst[:, :128] = (a[None, :] <= a[:, None]).astype(np.float32)
    maskT = (a[:, None] > a[None, :]).astype(np.float32)
    poolB = np.zeros((12, 128, 128), np.float32)
    for g, w in enumerate(POOL_WINDOWS):
        for ao in range(128):
            for i in range(w):
                b = ao + i
                if b < 128:
                    poolB[g, b, ao] += 1.0 / w
                else:
                    poolB[4 + g, b - 128, ao] += 1.0 / w
            poolB[g, ao, ao] -= 1.0
            cnt = min(128 - ao, w)
            for i in range(cnt):
                poolB[8 + g, ao + i, ao] += 1.0 / cnt
            poolB[8 + g, ao, ao] -= 1.0
    return dict(c_ident=ident, c_maskfirst=maskfirst, c_maskT=maskT, c_maskQ=np.ascontiguousarray(maskT.T), c_poolB=poolB)


_NC_CACHE = {}


def run(x, pool_w, pool_scale, w_q, w_kv, kv_norm_g, w_o, w_up, w_down,
        mix_pre_g, mix_post_g, mlp_pre_g, mlp_post_g, n_cores=None):
    f = lambda a: np.ascontiguousarray(np.asarray(a, dtype=np.float32))
    x = f(x)
    B, S, _ = x.shape
    gains = np.concatenate([f(mix_pre_g), f(mix_post_g), f(mlp_pre_g), f(mlp_post_g), f(kv_norm_g)[None, :], f(pool_scale)], axis=0)
    shared = dict(gains=np.ascontiguousarray(gains), pool_w=f(pool_w), w_q=f(w_q), w_kv=f(w_kv), w_o=f(w_o),
                  w_up=f(w_up), w_down=f(w_down))
    shared.update(host_consts())
    if S not in _NC_CACHE:
        _NC_CACHE[S] = build(S)
    nc = _NC_CACHE[S]
    in_maps = []
    for b in range(B):
        m = dict(shared)
        m["x"] = np.ascontiguousarray(x[b, ::-1, :])
        in_maps.append(m)
    res = run_bass_kernel_spmd(nc, in_maps, core_ids=list(range(B)))
    outs = [np.asarray(r["out"], dtype=np.float32)[::-1, :] for r in res.results]
    return np.ascontiguousarray(np.stack(outs, axis=0))


def kernel(**inputs):
    return run(**inputs)
```

```python
import numpy as np
from contextlib import ExitStack
import concourse.bass as bass
import concourse.mybir as mybir
from concourse.bass_utils import run_bass_kernel_spmd

F32 = mybir.dt.float32
BF16 = mybir.dt.bfloat16
ALU = mybir.AluOpType
AF = mybir.ActivationFunctionType

D = 1024
KC = 8
FF = 4096
FC = 32
EPS = 1e-6
POOL_WINDOWS = (2, 4, 8, 16)
ENGS = (("pe", "tensor"), ("act", "scalar"), ("dve", "vector"), ("pool", "gpsimd"), ("sp", "sync"))


class Sem:
    def __init__(self, h):
        self.h = h
        self.n = 0
        self.bn = 0


class Ring:
    def __init__(self, tiles):
        self.tiles = tiles
        self.rel = [[] for _ in tiles]
        self.k = 0

    def acquire(self):
        i = self.k % len(self.tiles)
        self.k += 1
        w = self.rel[i]
        self.rel[i] = []
        return i, self.tiles[i], w

    def release(self, i, *toks):
        self.rel[i].extend(t for t in toks if t is not None)


class Sched:
    def __init__(self, nc, es):
        self.nc = nc
        self.es = es
        self.sems = []
        self.q = {e: [] for e, _ in ENGS}
        self.esem = {e: self.newsem("s_" + e) for e, _ in ENGS}
        self.pending = {e: [] for e, _ in ENGS}

    def newsem(self, name):
        s = Sem(self.es.enter_context(self.nc.semaphore(name)))
        self.sems.append(s)
        return s

    def add(self, eng, fn, waits=(), sem=None, dma=False):
        if dma:
            assert sem is not None
        s = sem if sem is not None else self.esem[eng]
        amt = 16 if dma else 1
        s.n += amt
        ws = [w for w in waits if w is not None]
        if self.pending[eng]:
            ws = self.pending[eng] + ws
            self.pending[eng] = []
        self.q[eng].append((fn, ws, s, amt))
        return (s, s.n)

    def barrier(self):
        toks = [(s, s.n) for s in self.sems if s.n > s.bn]
        for s in self.sems:
            s.bn = s.n
        for e, _ in ENGS:
            self.pending[e] = self.pending[e] + toks

    def emit(self):
        with self.nc.Block() as block:
            for name, attr in ENGS:
                ops = self.q[name]

                def body(eng, ops=ops):
                    seen = {}
                    for fn, waits, s, amt in ops:
                        for (ws, wv) in waits:
                            if seen.get(id(ws), 0) < wv:
                                eng.wait_ge(ws.h, wv)
                                seen[id(ws)] = wv
                        ins = fn(eng)
                        ins.then_inc(s.h, amt)

                getattr(block, attr)(body)
        self.q = {e: [] for e, _ in ENGS}


def build(S):
    NT = S // 128
    assert S % 512 == 0
    nc = bass.Bass("TRN2", target_bir_lowering=False)
    dr = lambda n, s, d, k: nc.dram_tensor(n, s, d, kind=k).ap()
    x_in = dr("x", [S, D], F32, "ExternalInput")
    gains = dr("gains", [19, D], F32, "ExternalInput")
    pool_w = dr("pool_w", [2, 4, 256, 256], F32, "ExternalInput")
    w_q = dr("w_q", [2, D, D], F32, "ExternalInput")
    w_kv = dr("w_kv", [D, 2 * D], F32, "ExternalInput")
    w_o = dr("w_o", [2, D, D], F32, "ExternalInput")
    w_up = dr("w_up", [4, D, FF], F32, "ExternalInput")
    w_down = dr("w_down", [4, FF, D], F32, "ExternalInput")
    c_ident = dr("c_ident", [128, 128], F32, "ExternalInput")
    c_maskfirst = dr("c_maskfirst", [128, 1024], F32, "ExternalInput")
    c_maskT = dr("c_maskT", [128, 128], F32, "ExternalInput")
    c_negtri = dr("c_negtri", [128, 128], F32, "ExternalInput")
    c_poolB = dr("c_poolB", [12, 128, 128], F32, "ExternalInput")
    out = dr("out", [S, D], F32, "ExternalOutput")
    xs = dr("xs", [S, D], F32, "Internal")
    Vd = dr("Vd", [S + 1, D], F32, "Internal")
    dVd = dr("dVd", [S, D], BF16, "Internal")
    KTd = dr("KTd", [D, S], BF16, "Internal")
    QTd = dr("QTd", [D, S], BF16, "Internal")
    Od = dr("Od", [S, D], BF16, "Internal")

    ISQ = float(1.0 / np.sqrt(D))

    _uid = [0]

    def un(n):
        _uid[0] += 1
        return f"{n}_{_uid[0]}"

    with ExitStack() as ges:
        sc = Sched(nc, ges)
        add = sc.add
        gsb = lambda n, s, d: ges.enter_context(nc.sbuf_tensor(n, s, d))
        ident = gsb("ident", [128, 128], BF16)
        maskT = gsb("maskT", [128, 128], BF16)
        negtri = gsb("negtri", [128, 128], BF16)
        maskfirst = gsb("maskfirst", [128, 1024], BF16)
        zeros = gsb("zeros", [128, 1024], BF16)
        epst = gsb("epst", [128, 1], F32)
        stats = gsb("stats", [128, 3 * 8], F32)
        STAT = Ring([stats[:, 3 * i:3 * i + 3] for i in range(8)])
        s_const = sc.newsem("s_const")

        t_c = add("pool", lambda e: e.dma_start(out=ident[:], in_=c_ident), sem=s_const, dma=True)
        t_c = add("pool", lambda e: e.dma_start(out=maskT[:], in_=c_maskT), sem=s_const, dma=True)
        t_c = add("pool", lambda e: e.dma_start(out=negtri[:], in_=c_negtri), sem=s_const, dma=True)
        t_c = add("pool", lambda e: e.dma_start(out=maskfirst[:], in_=c_maskfirst), sem=s_const, dma=True)
        onesb = gsb("onesb", [128, 128], BF16)
        add("dve", lambda e: e.memset(zeros[:], 0.0))
        add("dve", lambda e: e.memset(onesb[:], 1.0))
        add("dve", lambda e: e.memset(epst[:], EPS))
        sc.barrier()

        def load_gain(tile, row, sem):
            return add("sp", lambda e: e.dma_start(out=tile[:], in_=gains[row:row + 1, :].partition_broadcast(128)),
                       sem=sem, dma=True)

        def rstd_chain(src_ap, t_src, extra_waits=()):
            si, st, sw = STAT.acquire()
            t_ms = add("act", lambda e: e.activation(out=junk[:], in_=src_ap, func=AF.Square, scale=ISQ,
                                                     accum_out=st[:, 0:1]), waits=[t_src] + sw + list(extra_waits))
            t_sd = add("act", lambda e: e.activation(out=st[:, 1:2], in_=st[:, 0:1], func=AF.Sqrt, bias=epst[:],
                                                     scale=1.0), waits=[t_ms])
            t_rs = add("dve", lambda e: e.reciprocal(out=st[:, 2:3], in_=st[:, 1:2]), waits=[t_sd])
            return si, st[:, 2:3], t_rs

        def transpose8(src_tile, t_src, TP, dst_view, dst_waits):
            pi, tp, pw_ = TP.acquire()

            def f(e):
                for kc in range(KC):
                    ins = e.transpose(out=tp[:, kc, :], in_=src_tile[:, kc * 128:(kc + 1) * 128], identity=ident[:])
                return ins
            t_T = add("pe", f, waits=[t_src] + pw_)
            t_ev = add("act", lambda e: e.activation(out=dst_view, in_=tp[:], func=AF.Copy), waits=[t_T] + list(dst_waits))
            TP.release(pi, t_ev)
            return t_T, t_ev

        def norm_T(src_rows, gtile, t_g, XT, HB, TP, dst_view, dst_waits, keep_x=False):
            xi, xt, xw = XT.acquire()
            t_ld = add("sp", lambda e: e.dma_start(out=xt[:], in_=src_rows), waits=xw, sem=XT.sems[xi], dma=True)
            si, rs, t_rs = rstd_chain(xt[:], t_ld)
            hi, hb, hw = HB.acquire()
            t_h = add("dve", lambda e: e.scalar_tensor_tensor(out=hb[:], in0=xt[:], scalar=rs, in1=gtile[:],
                                                              op0=ALU.mult, op1=ALU.mult), waits=[t_rs, t_ld, t_g] + hw)
            STAT.release(si, t_h)
            t_T, t_ev = transpose8(hb, t_h, TP, dst_view, dst_waits)
            HB.release(hi, t_T)
            if not keep_x:
                XT.release(xi, t_h)
            return xi, xt, t_ld, t_h, t_ev, hb, hi

        def norm_part(src_rows, gtile, t_g, XT, HB):
            xi, xt, xw = XT.acquire()
            t_ld = add("sp", lambda e: e.dma_start(out=xt[:], in_=src_rows), waits=xw, sem=XT.sems[xi], dma=True)
            si, rs, t_rs = rstd_chain(xt[:], t_ld)
            hi, hb, hw = HB.acquire()
            t_h = add("dve", lambda e: e.scalar_tensor_tensor(out=hb[:], in0=xt[:], scalar=rs, in1=gtile[:],
                                                              op0=ALU.mult, op1=ALU.mult), waits=[t_rs, t_ld, t_g] + hw)
            STAT.release(si, t_h)
            return dict(xi=xi, xt=xt, t_ld=t_ld, hi=hi, hb=hb, t_h=t_h)

        def T_part(nd, TP, HB, dst_view, dst_waits):
            t_T, t_ev = transpose8(nd["hb"], nd["t_h"], TP, dst_view, dst_waits)
            HB.release(nd["hi"], t_T)
            return t_ev

        def epilogue(y_ap, t_y, gtile, t_g, xr, t_xr, T2, dst_rows):
            si, rs, t_rs = rstd_chain(y_ap, t_y)
            ti, t2, tw = T2.acquire()
            t_t2 = add("dve", lambda e: e.scalar_tensor_tensor(out=t2[:], in0=y_ap, scalar=rs, in1=gtile[:],
                                                               op0=ALU.mult, op1=ALU.mult), waits=[t_rs, t_y, t_g] + tw)
            STAT.release(si, t_t2)
            t_add = add("pool", lambda e: e.tensor_tensor(out=t2[:], in0=t2[:], in1=xr[:], op=ALU.add), waits=[t_t2, t_xr])
            t_st = add("pool", lambda e: e.dma_start(out=dst_rows, in_=t2[:]), waits=[t_add], sem=T2.sems[ti], dma=True)
            T2.release(ti, t_st)
            return t_t2, t_st, t_add

        def mkring(es, name, n, shape, dtype, sems=False, psum=False):
            alloc = nc.psum_tensor if psum else nc.sbuf_tensor
            r = Ring([es.enter_context(alloc(un(f"{name}{i}"), shape, dtype)) for i in range(n)])
            if sems:
                r.sems = [namedsem(f"s_{name}{i}") for i in range(n)]
            return r

        _semcache = {}

        def namedsem(name):
            if name not in _semcache:
                _semcache[name] = sc.newsem(name)
            return _semcache[name]

        def phase_pool(l, src, dst, Wts=None):
            nonlocal junk
            with ExitStack() as es:
                sb = lambda n, s, d: es.enter_context(nc.sbuf_tensor(un(n), s, d))
                junk = sb("junk", [128, D], BF16)
                pw = sb("pw", [128, 4, 2, 256], BF16)
                PB = sb("PB", [128, 12, 128], BF16)
                gpre = sb("gpre", [128, D], F32)
                gpost = sb("gpost", [128, D], F32)
                psc = sb("psc", [128, D], F32)
                s_w = namedsem("s_w")
                for g in range(4):
                    t_w = add("pool", lambda e, g=g: e.dma_start(out=pw[:, g, :, :], in_=pool_w[l, g].rearrange("(j p) d -> p j d", p=128)),
                              sem=s_w, dma=True)
                t_w = add("pool", lambda e: e.dma_start(out=PB[:], in_=c_poolB.rearrange("m p a -> p m a")), sem=s_w, dma=True)
                t_w = load_gain(gpre, 0 + l, s_w)
                t_w = load_gain(gpost, 4 + l, s_w)
                t_w = load_gain(psc, 17 + l, s_w)
                XT = mkring(es, "xt", 2, [128, D], F32, sems=True)
                XR = mkring(es, "xr", 2, [128, D], F32, sems=True)
                HB = mkring(es, "hb", 3, [128, D], BF16)
                T1 = mkring(es, "t1", 2, [128, D], F32)
                T2 = mkring(es, "t2", 2, [128, D], F32, sems=True)
                YB = mkring(es, "yb", 2, [128, KC, 128], BF16)
                YP = mkring(es, "yp", 2, [128, KC, 128], F32, psum=True)
                MP = mkring(es, "mp", 2, [128, D], F32, psum=True)
                st = [dict() for _ in range(NT)]

                def A(it):
                    i = NT - 1 - it
                    d = st[it]
                    d["rows"] = slice(i * 128, (i + 1) * 128)
                    d.update(norm_part(src[d["rows"], :], gpre, t_w, XT, HB))
                    XT.release(d["xi"], d["t_h"])

                def B1(it):
                    d = st[it]
                    prev = st[it - 1] if it > 0 else None
                    hb = d["hb"]
                    yi, yp, yw = YP.acquire()

                    def f_mix(e):
                        for cc in range(KC):
                            g = cc // 2
                            if prev is None:
                                ins = e.matmul(yp[:, cc, :], lhsT=hb[:, cc * 128:(cc + 1) * 128], rhs=PB[:, 8 + g, :], start=True, stop=True)
                            else:
                                e.matmul(yp[:, cc, :], lhsT=hb[:, cc * 128:(cc + 1) * 128], rhs=PB[:, g, :], start=True, stop=False)
                                ins = e.matmul(yp[:, cc, :], lhsT=prev["hb"][:, cc * 128:(cc + 1) * 128], rhs=PB[:, 4 + g, :], start=False, stop=True)
                        return ins
                    t_mix = add("pe", f_mix, waits=[d["t_h"], t_w] + yw + ([prev["t_h"]] if prev else []))
                    if prev is not None:
                        HB.release(prev["hi"], t_mix)
                    if it == NT - 1:
                        HB.release(d["hi"], t_mix)
                    bi, yb, bw = YB.acquire()
                    t_yb = add("act", lambda e: e.activation(out=yb[:], in_=yp[:], func=AF.Copy), waits=[t_mix] + bw)
                    YP.release(yi, t_yb)
                    d.update(bi=bi, yb=yb, t_yb=t_yb)

                def B2(it):
                    d = st[it]
                    yb = d["yb"]
                    mi, mp, mw = MP.acquire()

                    def f_lin(e):
                        for g in range(4):
                            for j in range(2):
                                ins = e.matmul(mp[:, g * 256:(g + 1) * 256], lhsT=yb[:, 2 * g + j, :], rhs=pw[:, g, j, :],
                                               start=(j == 0), stop=(j == 1))
                        return ins
                    t_lin = add("pe", f_lin, waits=[d["t_yb"]] + mw)
                    YB.release(d["bi"], t_lin)
                    ui, t1, uw = T1.acquire()
                    t_t1 = add("dve", lambda e: e.tensor_tensor(out=t1[:], in0=mp[:], in1=psc[:], op=ALU.mult), waits=[t_lin] + uw)
                    MP.release(mi, t_t1)
                    d.update(ui=ui, t1=t1, t_t1=t_t1)
                    ri, xr, rw = XR.acquire()
                    rows = d["rows"]
                    t_xr = add("sp", lambda e: e.dma_start(out=xr[:], in_=src[rows, :]), waits=rw, sem=XR.sems[ri], dma=True)
                    d.update(ri=ri, xr=xr, t_xr=t_xr)

                def C(it):
                    d = st[it]
                    t_t2, t_st, t_add = epilogue(d["t1"][:], d["t_t1"], gpost, t_w, d["xr"], d["t_xr"], T2, dst[d["rows"], :])
                    T1.release(d["ui"], t_t2)
                    XR.release(d["ri"], t_add)
                    if Wts is not None and Wts["thunks"]:
                        Wts["thunks"].pop(0)()

                for n in range(NT + 3):
                    if n < NT:
                        A(n)
                    if 0 <= n - 1 < NT:
                        B1(n - 1)
                    if 0 <= n - 2 < NT:
                        B2(n - 2)
                    if 0 <= n - 3 < NT:
                        C(n - 3)
                sc.barrier()
                sc.emit()

        def load_w_rows(dst3, src2, nk, width, sem, chunk):
            srcv = src2.rearrange("(k p) f -> p k f", p=128)
            t = None
            for k0 in range(0, nk, chunk):
                t = add("pool", lambda e, k0=k0: e.dma_start(out=dst3[:, k0:k0 + chunk, :], in_=srcv[:, k0:k0 + chunk, :]),
                        sem=sem, dma=True)
            return t

        def mlp_weights(wes, l, part="all", Wts=None):
            sbw = lambda n, s, d: wes.enter_context(nc.sbuf_tensor(un(n), s, d))
            if Wts is None:
                Wts = dict(thunks=[], sem=namedsem("s_wmlp"))
            s_w = Wts["sem"]
            thunks = Wts["thunks"]
            if part in ("up", "all"):
                wup = sbw("wup", [128, KC, FF], BF16)
                Wts["wup"] = wup
                upv = w_up[l].rearrange("(k p) f -> p k f", p=128)
                for k0 in range(KC):
                    thunks.append(lambda k0=k0: add("pool", lambda e: e.dma_start(out=wup[:, k0:k0 + 1, :], in_=upv[:, k0:k0 + 1, :]),
                                                    sem=s_w, dma=True))
            if part in ("down", "all"):
                wdn = sbw("wdn", [128, FC, D], BF16)
                gpre = sbw("gpre", [128, D], F32)
                gpost = sbw("gpost", [128, D], F32)
                Wts.update(wdn=wdn, gpre=gpre, gpost=gpost)
                dnv = w_down[l].rearrange("(k p) f -> p k f", p=128)
                for k0 in range(0, FC, 4):
                    thunks.append(lambda k0=k0: add("pool", lambda e: e.dma_start(out=wdn[:, k0:k0 + 4, :], in_=dnv[:, k0:k0 + 4, :]),
                                                    sem=s_w, dma=True))
                thunks.append(lambda: load_gain(gpre, 8 + l, s_w))
                thunks.append(lambda: load_gain(gpost, 12 + l, s_w))
            return Wts

        def flush_w(Wts):
            while Wts["thunks"]:
                Wts["thunks"].pop(0)()

        def phase_mlp(l, src, dst, Wts):
            nonlocal junk
            TT = 256
            NS = TT // 128
            NTT = S // TT
            with ExitStack() as es:
                sb = lambda n, s, d: es.enter_context(nc.sbuf_tensor(un(n), s, d))
                junk = sb("junk", [128, D], BF16)
                flush_w(Wts)
                wup, wdn, gpre, gpost = Wts["wup"], Wts["wdn"], Wts["gpre"], Wts["gpost"]
                t_w = (Wts["sem"], Wts["sem"].n)
                XT = mkring(es, "xt", 6, [128, D], F32, sems=True)
                HB = mkring(es, "hb", 2, [128, D], BF16)
                T2 = mkring(es, "t2", 2, [128, D], F32, sems=True)
                R = mkring(es, "r", 3, [128, TT], BF16)
                hT = sb("hT", [128, KC, TT], BF16)
                uT = sb("uT", [128, FC, TT], BF16)
                TP = mkring(es, "tp", 2, [128, KC, 128], BF16, psum=True)
                UP = mkring(es, "up", 2, [128, 512], F32, psum=True)
                DN = mkring(es, "dn", 2, [128, D], F32, psum=True)
                xinfo = {}
                hT_rd = []
                uT_rd = []

                nds = {}

                def do_norm(t):
                    for s in range(NS):
                        rows = slice(t * TT + s * 128, t * TT + (s + 1) * 128)
                        nd = norm_part(src[rows, :], gpre, t_w, XT, HB)
                        nds[(t, s)] = nd
                        xinfo[(t, s)] = (nd["xi"], nd["xt"], nd["t_ld"])

                def do_T(t):
                    toks = []
                    for s in range(NS):
                        toks.append(T_part(nds.pop((t, s)), TP, HB, hT[:, :, s * 128:(s + 1) * 128], hT_rd))
                    return toks

                def do_up(t, t_hT):
                    last = None
                    tks = []
                    for fc in range(FC):
                        ui, up, uw = UP.acquire()

                        def f(e, fc=fc, up=up):
                            for kc in range(KC):
                                ins = e.matmul(up[:, 0:TT], lhsT=wup[:, kc, fc * 128:(fc + 1) * 128], rhs=hT[:, kc, :],
                                               start=(kc == 0), stop=(kc == KC - 1))
                            return ins
                        t_mm = add("pe", f, waits=list(t_hT) + [t_w] + uw)
                        ri, r, rw = R.acquire()
                        t_r = add("act", lambda e, r=r, up=up: e.activation(out=r[:], in_=up[:, 0:TT], func=AF.Relu), waits=[t_mm] + rw)
                        UP.release(ui, t_r)
                        t_u = add("dve", lambda e, fc=fc, r=r: e.tensor_tensor(out=uT[:, fc, :], in0=r[:], in1=r[:], op=ALU.mult),
                                  waits=[t_r] + (uT_rd if fc == 0 else []))
                        R.release(ri, t_u)
                        tks.append(t_u)
                        last = t_mm
                    return last, tks

                def do_down(t, t_us):
                    last = None
                    for s in range(NS):
                        rows = slice(t * TT + s * 128, t * TT + (s + 1) * 128)
                        di, dn, dw = DN.acquire()

                        def f(e, s=s, dn=dn):
                            for dh in range(2):
                                for fc in range(FC):
                                    ins = e.matmul(dn[:, dh * 512:(dh + 1) * 512], lhsT=uT[:, fc, s * 128:(s + 1) * 128],
                                                   rhs=wdn[:, fc, dh * 512:(dh + 1) * 512], start=(fc == 0), stop=(fc == FC - 1))
                            return ins
                        t_mm = add("pe", f, waits=[t_us[-1]] + dw)
                        xi, xt, t_ld = xinfo.pop((t, s))
                        t_t2, t_st, t_add = epilogue(dn[:], t_mm, gpost, t_w, xt, t_ld, T2, dst[rows, :])
                        DN.release(di, t_t2)
                        XT.release(xi, t_add)
                        last = t_mm
                    return last

                do_norm(0)
                t_hT = do_T(0)
                for t in range(NTT):
                    if t + 1 < NTT:
                        do_norm(t + 1)
                    t_upmm, t_us = do_up(t, t_hT)
                    hT_rd[:] = [t_upmm]
                    if t + 1 < NTT:
                        t_hT = do_T(t + 1)
                    t_dn = do_down(t, t_us)
                    uT_rd[:] = [t_dn]
                sc.barrier()
                sc.emit()

        def phase_proj(kind, j, src):
            nonlocal junk
            TT = 512
            NS = 4
            NTT = S // TT
            with ExitStack() as es:
                sb = lambda n, s, d: es.enter_context(nc.sbuf_tensor(un(n), s, d))
                junk = sb("junk", [128, D], BF16)
                s_w = namedsem("s_w")
                gpre = sb("gpre", [128, D], F32)
                if kind == "kv":
                    W = sb("wkv", [128, KC, 2 * D], BF16)
                    load_w_rows(W, w_kv, KC, 2 * D, s_w, 1)
                    t_w = load_gain(gpre, 16, s_w)
                    dstT = KTd
                else:
                    W = sb("wq", [128, KC, D], BF16)
                    load_w_rows(W, w_q[j], KC, D, s_w, 2)
                    t_w = load_gain(gpre, 2 + j, s_w)
                    dstT = QTd
                dstTv = dstT.rearrange("(h p) s -> p h s", p=128)
                XT = mkring(es, "xt", 3, [128, D], F32, sems=True)
                HB = mkring(es, "hb", 3, [128, D], BF16)
                HT = mkring(es, "hT", 2, [128, KC, TT], BF16)
                KS = mkring(es, "ks", 2, [128, KC, TT], BF16, sems=True)
                TP = mkring(es, "tp", 2, [128, KC, 128], BF16, psum=True)
                KP = mkring(es, "kp", 2, [128, 512], F32, psum=True)
                if kind == "kv":
                    VP = mkring(es, "vp", 2, [128, D], F32, psum=True)
                    VS = mkring(es, "vs", 3, [128, D], F32, sems=True)
                    VB = mkring(es, "vb", 2, [128, D], F32, sems=True)
                    DV = mkring(es, "dv", 2, [128, D], BF16, sems=True)
                    s_z = namedsem("s_vz")
                    add("sp", lambda e: e.dma_start(out=Vd[S:S + 1, 0:512], in_=zeros[0:1, :].bitcast(F32)), sem=s_z, dma=True)
                    vstate = dict(t_prev_vst=add("sp", lambda e: e.dma_start(out=Vd[S:S + 1, 512:1024], in_=zeros[0:1, :].bitcast(F32)), sem=s_z, dma=True))
                order = [(t, s_) for t in reversed(range(NTT)) for s_ in range(NS)]
                tiles = {}
                nds = {}

                def A(n):
                    t, s_ = order[n]
                    rows = slice(t * TT + s_ * 128, t * TT + (s_ + 1) * 128)
                    nd = norm_part(src[rows, :], gpre, t_w, XT, HB)
                    XT.release(nd["xi"], nd["t_h"])
                    nds[n] = nd

                def B(n):
                    t, s_ = order[n]
                    if s_ == 0:
                        hi_, hT, hw_ = HT.acquire()
                        tiles[t] = dict(hi=hi_, hT=hT, hw=hw_, toks=[], rd=[])
                    td = tiles[t]
                    t_ev = T_part(nds.pop(n), TP, HB, td["hT"][:, :, s_ * 128:(s_ + 1) * 128], td["hw"])
                    td["toks"].append(t_ev)

                def Cpart(t, q):
                    td = tiles[t]
                    hT, toks = td["hT"], td["toks"]
                    if q == 0:
                        ki, ks, kw = KS.acquire()
                        td.update(ki=ki, ks=ks, kw=kw)
                    ks = td["ks"]
                    for hp in (2 * q, 2 * q + 1):
                        pi, kp, pw_ = KP.acquire()

                        def f(e, hp=hp, kp=kp):
                            for kc in range(KC):
                                ins = e.matmul(kp[:], lhsT=W[:, kc, hp * 128:(hp + 1) * 128], rhs=hT[:, kc, :],
                                               start=(kc == 0), stop=(kc == KC - 1))
                            return ins
                        t_mm = add("pe", f, waits=toks + [t_w] + pw_)
                        t_kev = add("act", lambda e, hp=hp, kp=kp: e.activation(out=ks[:, hp, :], in_=kp[:], func=AF.Copy),
                                    waits=[t_mm] + (td["kw"] if hp == 0 else []))
                        KP.release(pi, t_kev)
                        td["rd"] = [t_mm]
                    if q == 3:
                        t_kst = add("pool", lambda e: e.dma_start(out=dstTv[:, :, t * TT:(t + 1) * TT], in_=ks[:]),
                                    waits=[t_kev], sem=KS.sems[td["ki"]], dma=True)
                        KS.release(td["ki"], t_kst)
                    if kind == "kv":
                        s_ = NS - 1 - q
                        p0 = t * TT + s_ * 128
                        vi, vp, vw = VP.acquire()

                        def f(e):
                            for dh in range(2):
                                for kc in range(KC):
                                    ins = e.matmul(vp[:, dh * 512:(dh + 1) * 512], lhsT=hT[:, kc, s_ * 128:(s_ + 1) * 128],
                                                   rhs=W[:, kc, D + dh * 512:D + (dh + 1) * 512], start=(kc == 0), stop=(kc == KC - 1))
                            return ins
                        t_mm = add("pe", f, waits=toks + [t_w] + vw)
                        td["rd"] = [t_mm]
                        si_, vs, sw_ = VS.acquire()
                        t_vev = add("act", lambda e: e.activation(out=vs[:], in_=vp[:], func=AF.Copy), waits=[t_mm] + sw_)
                        VP.release(vi, t_vev)
                        t_vst = add("pool", lambda e: e.dma_start(out=Vd[p0:p0 + 128, :], in_=vs[:]), waits=[t_vev],
                                    sem=VS.sems[si_], dma=True)
                        bi, vb, bw = VB.acquire()
                        t_vb = add("pool", lambda e: e.dma_start(out=vb[:], in_=Vd[p0 + 1:p0 + 129, :]),
                                   waits=[t_vst, vstate["t_prev_vst"]] + bw, sem=VB.sems[bi], dma=True)
                        vstate["t_prev_vst"] = t_vst
                        di, dv, dw = DV.acquire()
                        t_dv = add("dve", lambda e: e.tensor_tensor(out=dv[:], in0=vb[:], in1=vs[:], op=ALU.subtract),
                                   waits=[t_vb, t_vev] + dw)
                        VB.release(bi, t_dv)
                        VS.release(si_, t_dv, t_vst)
                        t_dst = add("pool", lambda e: e.dma_start(out=dVd[p0:p0 + 128, :], in_=dv[:]), waits=[t_dv],
                                    sem=DV.sems[di], dma=True)
                        DV.release(di, t_dst)
                    if q == 3:
                        HT.release(td["hi"], *td["rd"])

                NSUB = len(order)
                for n in range(NSUB + 1 + NS):
                    if n < NSUB:
                        A(n)
                    if 0 <= n - 1 < NSUB:
                        B(n - 1)
                    m = n - 1 - NS
                    if m >= 0 and m < NSUB:
                        tprev, q = order[m][0], m % NS
                        Cpart(tprev, q)
                sc.barrier()
                sc.emit()

        def phase_attn(Wts=None):
            with ExitStack() as es:
                KT = mkring(es, "KT", 2, [128, S], BF16, sems=True)
                QT = mkring(es, "QT", 2, [128, S], BF16, sems=True)
                DVt = mkring(es, "DVt", 2, [128, NT, 128], BF16, sems=True)
                VSH = mkring(es, "VSH", 2, [128, NT, 128], F32, sems=True)
                OSB = mkring(es, "OSB", 2, [128, NT, 128], BF16, sems=True)
                CW = 1024
                OM = mkring(es, "om", 3, [128, CW], F32)
                PR = mkring(es, "pr", 4, [128, CW], BF16)
                PTS = mkring(es, "pts", 3, [128, 8, 128], BF16)
                OTMP = mkring(es, "otmp", 2, [128, 128], F32)
                ZP = mkring(es, "zp", 2, [128, CW], F32, psum=True)
                PTP = mkring(es, "ptp", 2, [128, 8, 128], BF16, psum=True)
                OP = mkring(es, "op", 2, [128, 512], F32, psum=True)
                dVv = dVd.rearrange("(n p) d -> p n d", p=128)
                Vshv = Vd[1:S + 1, :].rearrange("(n p) d -> p n d", p=128)
                Odv = Od.rearrange("(n p) d -> p n d", p=128)
                items = []
                pairs = []

                def issue_pair_loads(hp):
                    pair = pairs[hp]
                    cols = pair["cols"]
                    ki, kt, kw = KT.acquire()
                    qi, qt, qw = QT.acquire()
                    di, dvt, dw = DVt.acquire()
                    vi, vsh, vw = VSH.acquire()
                    oi, osb, ow = OSB.acquire()
                    t_k = add("sp", lambda e: e.dma_start(out=kt[:], in_=KTd[cols, :]), waits=kw, sem=KT.sems[ki], dma=True)
                    t_q = add("sp", lambda e: e.dma_start(out=qt[:], in_=QTd[cols, :]), waits=qw, sem=QT.sems[qi], dma=True)
                    t_d = add("sp", lambda e: e.dma_start(out=dvt[:], in_=dVv[:, :, cols]), waits=dw, sem=DVt.sems[di], dma=True)
                    t_v = add("sp", lambda e: e.dma_start(out=vsh[:], in_=Vshv[:, :, cols]), waits=vw, sem=VSH.sems[vi], dma=True)
                    pair.update(kt=kt, qt=qt, dvt=dvt, vsh=vsh, osb=osb, t_k=t_k, t_q=t_q, t_d=t_d, t_v=t_v, ow=ow,
                                ki=ki, qi=qi, di=di, vi=vi, oi=oi)

                NPAIR = KC
                NQ = NT
                for hp in range(NPAIR):
                    pair = dict(hp=hp, cols=slice(hp * 128, (hp + 1) * 128))
                    pairs.append(pair)
                    for i in range(NT - NQ, NT):
                        grp = dict(pair=pair, i=i, n_left=0, first=True)
                        nch = (S - 128 * i + CW - 1) // CW
                        for c in range(nch):
                            k0 = 128 * i + CW * c
                            w = min(CW, S - k0)
                            for hd in range(2):
                                items.append(dict(pair=pair, grp=grp, i=i, c=c, k0=k0, w=w, hd=hd, last=(c == nch - 1)))
                                grp["n_left"] += 1
                issue_pair_loads(0)
                if NPAIR > 1:
                    issue_pair_loads(1)
                N = len(items)
                o_state = {}

                def st_QK(k):
                    it = items[k]
                    zi, zp, zw = ZP.acquire()
                    it["zi"], it["zp"] = zi, zp
                    hs = slice(it["hd"] * 64, it["hd"] * 64 + 64)
                    pr_, i, k0, w = it["pair"], it["i"], it["k0"], it["w"]
                    def f(e):
                        for j0 in range(0, w, 512):
                            wj = min(512, w - j0)
                            ins = e.matmul(zp[:, j0:j0 + wj], lhsT=pr_["qt"][hs, i * 128:(i + 1) * 128],
                                           rhs=pr_["kt"][hs, k0 + j0:k0 + j0 + wj], start=True, stop=True)
                        return ins
                    it["t_qk"] = add("pe", f, waits=[pr_["t_k"], pr_["t_q"]] + zw)

                def st_sig(k):
                    it = items[k]
                    mi, om, mw = OM.acquire()
                    it["mi"], it["om"] = mi, om
                    zp, w = it["zp"], it["w"]
                    def f(e):
                        for j0 in range(0, w, 512):
                            wj = min(512, w - j0)
                            ins = e.activation(out=om[:, j0:j0 + wj], in_=zp[:, j0:j0 + wj], func=AF.Sigmoid, scale=-0.125)
                        return ins
                    it["t_sig"] = add("act", f, waits=[it["t_qk"]] + mw)
                    ZP.release(it["zi"], it["t_sig"])

                def st_scan(k):
                    it = items[k]
                    pi, p, pw_ = PR.acquire()
                    it["pi"], it["p"] = pi, p
                    om, w = it["om"], it["w"]
                    if it["c"] == 0:
                        d1 = maskfirst[:, 0:w]
                        init = 1.0
                        cw = []
                    else:
                        prev = items[k - 2]
                        d1 = zeros[:, 0:w]
                        init = prev["p"][:, prev["w"] - 1:prev["w"]]
                        cw = [prev["t_scan"]]
                    it["t_scan"] = add("dve", lambda e: e.tensor_tensor_scan(out=p[:, 0:w], data0=om[:, 0:w], data1=d1, initial=init,
                                                                            op0=ALU.mult, op1=ALU.max),
                                       waits=[it["t_sig"]] + pw_ + cw)
                    OM.release(it["mi"], it["t_scan"])
                    it["t_p"] = it["t_scan"]

                def st_T(k):
                    it = items[k]
                    ti, tp, tw = PTP.acquire()
                    it["ti"], it["tp"] = ti, tp
                    p, nsub = it["p"], it["w"] // 128

                    def f(e):
                        for sbk in range(nsub):
                            ins = e.transpose(out=tp[:, sbk, :], in_=p[:, sbk * 128:(sbk + 1) * 128], identity=ident[:])
                        return ins
                    it["t_T"] = add("pe", f, waits=[it["t_p"]] + tw)
                    PR.release(it["pi"], it["t_T"])

                def st_evac(k):
                    it = items[k]
                    si_, pts, sw_ = PTS.acquire()
                    it["si"], it["pts"] = si_, pts
                    tp, nsub = it["tp"], it["w"] // 128
                    if True:
                        t_e = add("act", lambda e: e.activation(out=pts[:, 0:nsub, :], in_=tp[:, 0:nsub, :], func=AF.Copy),
                                  waits=[it["t_T"]] + sw_)
                    else:
                        t_e = add("dve", lambda e: e.tensor_copy(out=pts[:, 0:nsub, :], in_=tp[:, 0:nsub, :]),
                                  waits=[it["t_T"]] + sw_)
                    toks = [t_e]
                    it["t_ev"] = toks
                    PTP.release(it["ti"], *toks)

                def st_AV(k):
                    it = items[k]
                    grp, pr_ = it["grp"], it["pair"]
                    gw = []
                    if grp["first"]:
                        grp["first"] = False
                        oi, op, ow = OP.acquire()
                        grp["oi"], grp["op"] = oi, op
                        gw = ow
                    op = grp["op"]
                    pts, nsub, hd, k0 = it["pts"], it["w"] // 128, it["hd"], it["k0"]
                    oc = slice(hd * 64, hd * 64 + 64)
                    c, last = it["c"], it["last"]

                    i_q = it["i"]

                    def f(e):
                        if c == 0:
                            e.matmul(op[:, oc], lhsT=negtri[:], rhs=pr_["dvt"][:, i_q, oc], start=(hd == 0), stop=False,
                                     skip_group_check=True)
                        for sbk in range(nsub):
                            ins = e.matmul(op[:, oc], lhsT=pts[:, sbk, :], rhs=pr_["dvt"][:, k0 // 128 + sbk, oc],
                                           start=False, stop=(last and sbk == nsub - 1 and hd == 1),
                                           skip_group_check=True)
                        return ins
                    t_av = add("pe", f, waits=it["t_ev"] + [pr_["t_d"]] + gw)
                    PTS.release(it["si"], t_av)
                    grp["n_left"] -= 1
                    if grp["n_left"] == 0:
                        i = it["i"]
                        osb, vsh = pr_["osb"], pr_["vsh"]
                        mi_, otmp, mw_ = OTMP.acquire()
                        t_oa = add("act", lambda e: e.activation(out=otmp[:], in_=op[:, 0:128], func=AF.Copy), waits=[t_av] + mw_)
                        OP.release(grp["oi"], t_oa)
                        t_o = add("pool", lambda e: e.tensor_tensor(out=osb[:, i, :], in0=otmp[:], in1=vsh[:, i, :], op=ALU.add),
                                  waits=[t_oa, pr_["t_v"]] + pr_["ow"])
                        OTMP.release(mi_, t_o)
                        if i == NT - 1:
                            cols = pr_["cols"]
                            t_st = add("sp", lambda e: e.dma_start(out=Odv[:, :, cols], in_=osb[:]), waits=[t_o],
                                       sem=OSB.sems[pr_["oi"]], dma=True)
                            OSB.release(pr_["oi"], t_st)
                            KT.release(pr_["ki"], t_av)
                            QT.release(pr_["qi"], t_av)
                            DVt.release(pr_["di"], t_av)
                            VSH.release(pr_["vi"], t_o)
                            if pr_["hp"] + 2 < NPAIR:
                                issue_pair_loads(pr_["hp"] + 2)
                            if Wts is not None and Wts["thunks"]:
                                Wts["thunks"].pop(0)()

                for n in range(N + 5):
                    if n < N:
                        st_QK(n)
                        st_sig(n)
                        st_scan(n)
                    if 0 <= n - 3 < N:
                        st_T(n - 3)
                        st_evac(n - 3)
                    if 0 <= n - 5 < N:
                        st_AV(n - 5)
                sc.barrier()
                sc.emit()

        def phase_oproj(j, src, dst, Wts=None):
            nonlocal junk
            with ExitStack() as es:
                sb = lambda n, s, d: es.enter_context(nc.sbuf_tensor(un(n), s, d))
                junk = sb("junk", [128, D], BF16)
                s_w = namedsem("s_w")
                wo = sb("wo", [128, KC, D], BF16)
                gpost = sb("gpost", [128, D], F32)
                load_w_rows(wo, w_o[j], KC, D, s_w, 2)
                t_w = load_gain(gpost, 4 + 2 + j, s_w)
                OB = mkring(es, "ob", 3, [128, D], BF16, sems=True)
                XR = mkring(es, "xr", 4, [128, D], F32, sems=True)
                OT = mkring(es, "ot", 3, [128, KC, 128], BF16)
                T2 = mkring(es, "t2", 2, [128, D], F32, sems=True)
                TP = mkring(es, "tp", 2, [128, KC, 128], BF16, psum=True)
                YP = mkring(es, "yp", 2, [128, D], F32, psum=True)
                st = [dict() for _ in range(NT)]

                def A(i):
                    d = st[i]
                    rows = slice(i * 128, (i + 1) * 128)
                    bi, ob, bw = OB.acquire()
                    t_ob = add("sp", lambda e: e.dma_start(out=ob[:], in_=Od[rows, :]), waits=bw, sem=OB.sems[bi], dma=True)
                    xi, xr, xw = XR.acquire()
                    t_xr = add("sp", lambda e: e.dma_start(out=xr[:], in_=src[rows, :]), waits=xw, sem=XR.sems[xi], dma=True)
                    ti, ot, tw = OT.acquire()
                    t_T, t_ev = transpose8(ob, t_ob, TP, ot[:], tw)
                    OB.release(bi, t_T)
                    d.update(rows=rows, xi=xi, xr=xr, t_xr=t_xr, ti=ti, ot=ot, t_ev=t_ev)

                def B(i):
                    d = st[i]
                    ot = d["ot"]
                    yi, yp, yw = YP.acquire()

                    def f(e):
                        for dh in range(2):
                            for kc in range(KC):
                                ins = e.matmul(yp[:, dh * 512:(dh + 1) * 512], lhsT=ot[:, kc, :], rhs=wo[:, kc, dh * 512:(dh + 1) * 512],
                                               start=(kc == 0), stop=(kc == KC - 1))
                        return ins
                    t_mm = add("pe", f, waits=[d["t_ev"], t_w] + yw)
                    OT.release(d["ti"], t_mm)
                    t_t2, t_st, t_add = epilogue(yp[:], t_mm, gpost, t_w, d["xr"], d["t_xr"], T2, dst[d["rows"], :])
                    YP.release(yi, t_t2)
                    XR.release(d["xi"], t_add)
                    if Wts is not None and Wts["thunks"]:
                        Wts["thunks"].pop(0)()

                for n in range(NT + 1):
                    if n < NT:
                        A(n)
                    if 0 <= n - 1 < NT:
                        B(n - 1)
                sc.barrier()
                sc.emit()

        junk = None
        sc.emit()
        def pool_mlp(l, src):
            with ExitStack() as wes:
                Wts = mlp_weights(wes, l)
                phase_pool(l, src, xs, Wts)
                phase_mlp(l, xs, xs, Wts)

        def attn_oproj_mlp(j):
            with ExitStack() as wes:
                Wts = mlp_weights(wes, 2 + j, "up")
                phase_attn(Wts)
                flush_w(Wts)
                with ExitStack() as wes2:
                    mlp_weights(wes2, 2 + j, "down", Wts)
                    phase_oproj(j, xs, xs, Wts)
                    phase_mlp(2 + j, xs, out if j == 1 else xs, Wts)

        plist = [lambda: pool_mlp(0, x_in), lambda: pool_mlp(1, xs), lambda: phase_proj("kv", 0, xs)]
        for j in range(2):
            plist += [lambda j=j: phase_proj("q", j, xs), lambda j=j: attn_oproj_mlp(j)]
        for ph in plist:
            ph()
        add("sp", lambda e: e.nop())
        add("act", lambda e: e.nop())
        sc.emit()
    return nc


def host_consts():
    ident = np.eye(128, dtype=np.float32)
    a = np.arange(128)
    maskfirst = np.zeros((128, 1024), np.float32)
    maskfirst[:, :128] = (a[None, :] <= a[:, None]).astype(np.float32)
    maskT = (a[:, None] > a[None, :]).astype(np.float32)
    poolB = np.zeros((12, 128, 128), np.float32)
    for g, w in enumerate(POOL_WINDOWS):
        for ao in range(128):
            for i in range(w):
                b = ao + i
                if b < 128:
                    poolB[g, b, ao] += 1.0 / w
                else:
                    poolB[4 + g, b - 128, ao] += 1.0 / w
            poolB[g, ao, ao] -= 1.0
            cnt = min(128 - ao, w)
            for i in range(cnt):
                poolB[8 + g, ao + i, ao] += 1.0 / cnt
            poolB[8 + g, ao, ao] -= 1.0
    return dict(c_ident=ident, c_maskfirst=maskfirst, c_maskT=maskT, c_negtri=np.ascontiguousarray(maskT - 1.0), c_poolB=poolB)


_NC_CACHE = {}


def run(x, pool_w, pool_scale, w_q, w_kv, kv_norm_g, w_o, w_up, w_down,
        mix_pre_g, mix_post_g, mlp_pre_g, mlp_post_g, n_cores=None):
    f = lambda a: np.ascontiguousarray(np.asarray(a, dtype=np.float32))
    x = f(x)
    B, S, _ = x.shape
    gains = np.concatenate([f(mix_pre_g), f(mix_post_g), f(mlp_pre_g), f(mlp_post_g), f(kv_norm_g)[None, :], f(pool_scale)], axis=0)
    shared = dict(gains=np.ascontiguousarray(gains), pool_w=f(pool_w), w_q=f(w_q), w_kv=f(w_kv), w_o=f(w_o),
                  w_up=f(w_up), w_down=f(w_down))
    shared.update(host_consts())
    if S not in _NC_CACHE:
        _NC_CACHE[S] = build(S)
    nc = _NC_CACHE[S]
    in_maps = []
    for b in range(B):
        m = dict(shared)
        m["x"] = np.ascontiguousarray(x[b, ::-1, :])
        in_maps.append(m)
    res = run_bass_kernel_spmd(nc, in_maps, core_ids=list(range(B)))
    outs = [np.asarray(r["out"], dtype=np.float32)[::-1, :] for r in res.results]
    return np.ascontiguousarray(np.stack(outs, axis=0))


def kernel(**inputs):
    return run(**inputs)
```

```python
import numpy as np
from contextlib import ExitStack
import concourse.bass as bass
import concourse.mybir as mybir
from concourse.bass_utils import run_bass_kernel_spmd

F32 = mybir.dt.float32
BF16 = mybir.dt.bfloat16
ALU = mybir.AluOpType
AF = mybir.ActivationFunctionType

D = 1024
KC = 8
FF = 4096
FC = 32
EPS = 1e-6
POOL_WINDOWS = (2, 4, 8, 16)
ENGS = (("pe", "tensor"), ("act", "scalar"), ("dve", "vector"), ("pool", "gpsimd"), ("sp", "sync"))


class Sem:
    def __init__(self, h):
        self.h = h
        self.n = 0
        self.bn = 0


class Ring:
    def __init__(self, tiles):
        self.tiles = tiles
        self.rel = [[] for _ in tiles]
        self.k = 0

    def acquire(self):
        i = self.k % len(self.tiles)
        self.k += 1
        w = self.rel[i]
        self.rel[i] = []
        return i, self.tiles[i], w

    def release(self, i, *toks):
        self.rel[i].extend(t for t in toks if t is not None)


class Sched:
    def __init__(self, nc, es):
        self.nc = nc
        self.es = es
        self.sems = []
        self.q = {e: [] for e, _ in ENGS}
        self.esem = {e: self.newsem("s_" + e) for e, _ in ENGS}
        self.pending = {e: [] for e, _ in ENGS}

    def newsem(self, name):
        s = Sem(self.es.enter_context(self.nc.semaphore(name)))
        self.sems.append(s)
        return s

    def add(self, eng, fn, waits=(), sem=None, dma=False):
        if dma:
            assert sem is not None
        s = sem if sem is not None else self.esem[eng]
        amt = 16 if dma else 1
        s.n += amt
        ws = [w for w in waits if w is not None]
        if self.pending[eng]:
            ws = self.pending[eng] + ws
            self.pending[eng] = []
        self.q[eng].append((fn, ws, s, amt))
        return (s, s.n)

    def barrier(self):
        toks = [(s, s.n) for s in self.sems if s.n > s.bn]
        for s in self.sems:
            s.bn = s.n
        for e, _ in ENGS:
            self.pending[e] = self.pending[e] + toks

    def emit(self):
        with self.nc.Block() as block:
            for name, attr in ENGS:
                ops = self.q[name]

                def body(eng, ops=ops):
                    seen = {}
                    for fn, waits, s, amt in ops:
                        for (ws, wv) in waits:
                            if seen.get(id(ws), 0) < wv:
                                eng.wait_ge(ws.h, wv)
                                seen[id(ws)] = wv
                        ins = fn(eng)
                        ins.then_inc(s.h, amt)

                getattr(block, attr)(body)
        self.q = {e: [] for e, _ in ENGS}


def build(S):
    NT = S // 128
    assert S % 512 == 0
    nc = bass.Bass("TRN2", target_bir_lowering=False)
    dr = lambda n, s, d, k: nc.dram_tensor(n, s, d, kind=k).ap()
    x_in = dr("x", [S, D], F32, "ExternalInput")
    gains = dr("gains", [19, D], F32, "ExternalInput")
    pool_w = dr("pool_w", [2, 4, 256, 256], F32, "ExternalInput")
    w_q = dr("w_q", [2, D, D], F32, "ExternalInput")
    w_kv = dr("w_kv", [D, 2 * D], F32, "ExternalInput")
    w_o = dr("w_o", [2, D, D], F32, "ExternalInput")
    w_up = dr("w_up", [4, D, FF], F32, "ExternalInput")
    w_down = dr("w_down", [4, FF, D], F32, "ExternalInput")
    c_ident = dr("c_ident", [128, 128], F32, "ExternalInput")
    c_maskfirst = dr("c_maskfirst", [128, 1024], F32, "ExternalInput")
    c_maskT = dr("c_maskT", [128, 128], F32, "ExternalInput")
    c_negtri = dr("c_negtri", [128, 128], F32, "ExternalInput")
    c_poolB = dr("c_poolB", [12, 128, 128], F32, "ExternalInput")
    out = dr("out", [S, D], F32, "ExternalOutput")
    xs = dr("xs", [S, D], F32, "Internal")
    Vd = dr("Vd", [S + 1, D], F32, "Internal")
    dVd = dr("dVd", [S, D], BF16, "Internal")
    KTd = dr("KTd", [D, S], BF16, "Internal")
    QTd = dr("QTd", [D, S], BF16, "Internal")
    Od = dr("Od", [S, D], BF16, "Internal")

    ISQ = float(1.0 / np.sqrt(D))

    _uid = [0]

    def un(n):
        _uid[0] += 1
        return f"{n}_{_uid[0]}"

    with ExitStack() as ges:
        sc = Sched(nc, ges)
        add = sc.add
        gsb = lambda n, s, d: ges.enter_context(nc.sbuf_tensor(n, s, d))
        ident = gsb("ident", [128, 128], BF16)
        maskT = gsb("maskT", [128, 128], BF16)
        negtri = gsb("negtri", [128, 128], BF16)
        maskfirst = gsb("maskfirst", [128, 1024], BF16)
        zeros = gsb("zeros", [128, 1024], BF16)
        epst = gsb("epst", [128, 1], F32)
        stats = gsb("stats", [128, 3 * 8], F32)
        STAT = Ring([stats[:, 3 * i:3 * i + 3] for i in range(8)])
        s_const = sc.newsem("s_const")

        t_c = add("pool", lambda e: e.dma_start(out=ident[:], in_=c_ident), sem=s_const, dma=True)
        t_c = add("pool", lambda e: e.dma_start(out=maskT[:], in_=c_maskT), sem=s_const, dma=True)
        t_c = add("pool", lambda e: e.dma_start(out=negtri[:], in_=c_negtri), sem=s_const, dma=True)
        t_c = add("pool", lambda e: e.dma_start(out=maskfirst[:], in_=c_maskfirst), sem=s_const, dma=True)
        onesb = gsb("onesb", [128, 128], BF16)
        add("dve", lambda e: e.memset(zeros[:], 0.0))
        add("dve", lambda e: e.memset(onesb[:], 1.0))
        add("dve", lambda e: e.memset(epst[:], EPS))
        sc.barrier()

        def load_gain(tile, row, sem):
            return add("pool", lambda e: e.dma_start(out=tile[:], in_=gains[row:row + 1, :].partition_broadcast(128)),
                       sem=sem, dma=True)

        def rstd_chain(src_ap, t_src, extra_waits=()):
            si, st, sw = STAT.acquire()
            t_ms = add("act", lambda e: e.activation(out=junk[:], in_=src_ap, func=AF.Square, scale=ISQ,
                                                     accum_out=st[:, 0:1]), waits=[t_src] + sw + list(extra_waits))
            t_sd = add("act", lambda e: e.activation(out=st[:, 1:2], in_=st[:, 0:1], func=AF.Sqrt, bias=epst[:],
                                                     scale=1.0), waits=[t_ms])
            t_rs = add("dve", lambda e: e.reciprocal(out=st[:, 2:3], in_=st[:, 1:2]), waits=[t_sd])
            return si, st[:, 2:3], t_rs

        def transpose8(src_tile, t_src, TP, dst_view, dst_waits):
            pi, tp, pw_ = TP.acquire()

            def f(e):
                for kc in range(KC):
                    ins = e.transpose(out=tp[:, kc, :], in_=src_tile[:, kc * 128:(kc + 1) * 128], identity=ident[:])
                return ins
            t_T = add("pe", f, waits=[t_src] + pw_)
            t_ev = add("act", lambda e: e.activation(out=dst_view, in_=tp[:], func=AF.Copy), waits=[t_T] + list(dst_waits))
            TP.release(pi, t_ev)
            return t_T, t_ev

        def norm_T(src_rows, gtile, t_g, XT, HB, TP, dst_view, dst_waits, keep_x=False):
            xi, xt, xw = XT.acquire()
            t_ld = add("sp", lambda e: e.dma_start(out=xt[:], in_=src_rows), waits=xw, sem=XT.sems[xi], dma=True)
            si, rs, t_rs = rstd_chain(xt[:], t_ld)
            hi, hb, hw = HB.acquire()
            t_h = add("dve", lambda e: e.scalar_tensor_tensor(out=hb[:], in0=xt[:], scalar=rs, in1=gtile[:],
                                                              op0=ALU.mult, op1=ALU.mult), waits=[t_rs, t_ld, t_g] + hw)
            STAT.release(si, t_h)
            t_T, t_ev = transpose8(hb, t_h, TP, dst_view, dst_waits)
            HB.release(hi, t_T)
            if not keep_x:
                XT.release(xi, t_h)
            return xi, xt, t_ld, t_h, t_ev, hb, hi

        def norm_part(src_rows, gtile, t_g, XT, HB):
            xi, xt, xw = XT.acquire()
            t_ld = add("sp", lambda e: e.dma_start(out=xt[:], in_=src_rows), waits=xw, sem=XT.sems[xi], dma=True)
            si, rs, t_rs = rstd_chain(xt[:], t_ld)
            hi, hb, hw = HB.acquire()
            t_h = add("dve", lambda e: e.scalar_tensor_tensor(out=hb[:], in0=xt[:], scalar=rs, in1=gtile[:],
                                                              op0=ALU.mult, op1=ALU.mult), waits=[t_rs, t_ld, t_g] + hw)
            STAT.release(si, t_h)
            return dict(xi=xi, xt=xt, t_ld=t_ld, hi=hi, hb=hb, t_h=t_h)

        def T_part(nd, TP, HB, dst_view, dst_waits):
            t_T, t_ev = transpose8(nd["hb"], nd["t_h"], TP, dst_view, dst_waits)
            HB.release(nd["hi"], t_T)
            return t_ev

        def epilogue(y_ap, t_y, gtile, t_g, xr, t_xr, T2, dst_rows):
            si, rs, t_rs = rstd_chain(y_ap, t_y)
            ti, t2, tw = T2.acquire()
            t_t2 = add("dve", lambda e: e.scalar_tensor_tensor(out=t2[:], in0=y_ap, scalar=rs, in1=gtile[:],
                                                               op0=ALU.mult, op1=ALU.mult), waits=[t_rs, t_y, t_g] + tw)
            STAT.release(si, t_t2)
            t_add = add("pool", lambda e: e.tensor_tensor(out=t2[:], in0=t2[:], in1=xr[:], op=ALU.add), waits=[t_t2, t_xr])
            t_st = add("pool", lambda e: e.dma_start(out=dst_rows, in_=t2[:]), waits=[t_add], sem=T2.sems[ti], dma=True)
            T2.release(ti, t_st)
            return t_t2, t_st, t_add

        def mkring(es, name, n, shape, dtype, sems=False, psum=False):
            alloc = nc.psum_tensor if psum else nc.sbuf_tensor
            r = Ring([es.enter_context(alloc(un(f"{name}{i}"), shape, dtype)) for i in range(n)])
            if sems:
                r.sems = [namedsem(f"s_{name}{i}") for i in range(n)]
            return r

        _semcache = {}

        def namedsem(name):
            if name not in _semcache:
                _semcache[name] = sc.newsem(name)
            return _semcache[name]

        def phase_pool(l, src, dst, Wts=None):
            nonlocal junk
            with ExitStack() as es:
                sb = lambda n, s, d: es.enter_context(nc.sbuf_tensor(un(n), s, d))
                junk = sb("junk", [128, D], BF16)
                pw = sb("pw", [128, 4, 2, 256], BF16)
                PB = sb("PB", [128, 12, 128], BF16)
                gpre = sb("gpre", [128, D], F32)
                gpost = sb("gpost", [128, D], F32)
                psc = sb("psc", [128, D], F32)
                s_w = namedsem("s_w")
                for g in range(4):
                    t_w = add("pool", lambda e, g=g: e.dma_start(out=pw[:, g, :, :], in_=pool_w[l, g].rearrange("(j p) d -> p j d", p=128)),
                              sem=s_w, dma=True)
                t_w = add("pool", lambda e: e.dma_start(out=PB[:], in_=c_poolB.rearrange("m p a -> p m a")), sem=s_w, dma=True)
                t_w = load_gain(gpre, 0 + l, s_w)
                t_w = load_gain(gpost, 4 + l, s_w)
                t_w = load_gain(psc, 17 + l, s_w)
                XT = mkring(es, "xt", 2, [128, D], F32, sems=True)
                XR = mkring(es, "xr", 2, [128, D], F32, sems=True)
                HB = mkring(es, "hb", 3, [128, D], BF16)
                T1 = mkring(es, "t1", 2, [128, D], F32)
                T2 = mkring(es, "t2", 2, [128, D], F32, sems=True)
                YB = mkring(es, "yb", 2, [128, KC, 128], BF16)
                YP = mkring(es, "yp", 2, [128, KC, 128], F32, psum=True)
                MP = mkring(es, "mp", 2, [128, D], F32, psum=True)
                st = [dict() for _ in range(NT)]

                def A(it):
                    i = NT - 1 - it
                    d = st[it]
                    d["rows"] = slice(i * 128, (i + 1) * 128)
                    d.update(norm_part(src[d["rows"], :], gpre, t_w, XT, HB))
                    XT.release(d["xi"], d["t_h"])

                def B1(it):
                    d = st[it]
                    prev = st[it - 1] if it > 0 else None
                    hb = d["hb"]
                    yi, yp, yw = YP.acquire()

                    def f_mix(e):
                        for cc in range(KC):
                            g = cc // 2
                            if prev is None:
                                ins = e.matmul(yp[:, cc, :], lhsT=hb[:, cc * 128:(cc + 1) * 128], rhs=PB[:, 8 + g, :], start=True, stop=True)
                            else:
                                e.matmul(yp[:, cc, :], lhsT=hb[:, cc * 128:(cc + 1) * 128], rhs=PB[:, g, :], start=True, stop=False)
                                ins = e.matmul(yp[:, cc, :], lhsT=prev["hb"][:, cc * 128:(cc + 1) * 128], rhs=PB[:, 4 + g, :], start=False, stop=True)
                        return ins
                    t_mix = add("pe", f_mix, waits=[d["t_h"], t_w] + yw + ([prev["t_h"]] if prev else []))
                    if prev is not None:
                        HB.release(prev["hi"], t_mix)
                    if it == NT - 1:
                        HB.release(d["hi"], t_mix)
                    bi, yb, bw = YB.acquire()
                    t_yb = add("act", lambda e: e.activation(out=yb[:], in_=yp[:], func=AF.Copy), waits=[t_mix] + bw)
                    YP.release(yi, t_yb)
                    d.update(bi=bi, yb=yb, t_yb=t_yb)

                def B2(it):
                    d = st[it]
                    yb = d["yb"]
                    mi, mp, mw = MP.acquire()

                    def f_lin(e):
                        for g in range(4):
                            for j in range(2):
                                ins = e.matmul(mp[:, g * 256:(g + 1) * 256], lhsT=yb[:, 2 * g + j, :], rhs=pw[:, g, j, :],
                                               start=(j == 0), stop=(j == 1))
                        return ins
                    t_lin = add("pe", f_lin, waits=[d["t_yb"]] + mw)
                    YB.release(d["bi"], t_lin)
                    ui, t1, uw = T1.acquire()
                    t_t1 = add("dve", lambda e: e.tensor_tensor(out=t1[:], in0=mp[:], in1=psc[:], op=ALU.mult), waits=[t_lin] + uw)
                    MP.release(mi, t_t1)
                    d.update(ui=ui, t1=t1, t_t1=t_t1)
                    ri, xr, rw = XR.acquire()
                    rows = d["rows"]
                    t_xr = add("sp", lambda e: e.dma_start(out=xr[:], in_=src[rows, :]), waits=rw, sem=XR.sems[ri], dma=True)
                    d.update(ri=ri, xr=xr, t_xr=t_xr)

                def C(it):
                    d = st[it]
                    t_t2, t_st, t_add = epilogue(d["t1"][:], d["t_t1"], gpost, t_w, d["xr"], d["t_xr"], T2, dst[d["rows"], :])
                    T1.release(d["ui"], t_t2)
                    XR.release(d["ri"], t_add)
                    if Wts is not None and Wts["thunks"]:
                        Wts["thunks"].pop(0)()

                for n in range(NT + 3):
                    if n < NT:
                        A(n)
                    if 0 <= n - 1 < NT:
                        B1(n - 1)
                    if 0 <= n - 2 < NT:
                        B2(n - 2)
                    if 0 <= n - 3 < NT:
                        C(n - 3)
                sc.barrier()
                sc.emit()

        def load_w_rows(dst3, src2, nk, width, sem, chunk):
            srcv = src2.rearrange("(k p) f -> p k f", p=128)
            t = None
            for k0 in range(0, nk, chunk):
                t = add("pool", lambda e, k0=k0: e.dma_start(out=dst3[:, k0:k0 + chunk, :], in_=srcv[:, k0:k0 + chunk, :]),
                        sem=sem, dma=True)
            return t

        def mlp_weights(wes, l, part="all", Wts=None):
            sbw = lambda n, s, d: wes.enter_context(nc.sbuf_tensor(un(n), s, d))
            if Wts is None:
                Wts = dict(thunks=[], sem=namedsem("s_wmlp"))
            s_w = Wts["sem"]
            thunks = Wts["thunks"]
            if part in ("up", "all"):
                wup = sbw("wup", [128, KC, FF], BF16)
                Wts["wup"] = wup
                upv = w_up[l].rearrange("(k p) f -> p k f", p=128)
                for k0 in range(KC):
                    thunks.append(lambda k0=k0: add("pool", lambda e: e.dma_start(out=wup[:, k0:k0 + 1, :], in_=upv[:, k0:k0 + 1, :]),
                                                    sem=s_w, dma=True))
            if part in ("down", "all"):
                wdn = sbw("wdn", [128, FC, D], BF16)
                gpre = sbw("gpre", [128, D], F32)
                gpost = sbw("gpost", [128, D], F32)
                Wts.update(wdn=wdn, gpre=gpre, gpost=gpost)
                dnv = w_down[l].rearrange("(k p) f -> p k f", p=128)
                for k0 in range(0, FC, 4):
                    thunks.append(lambda k0=k0: add("pool", lambda e: e.dma_start(out=wdn[:, k0:k0 + 4, :], in_=dnv[:, k0:k0 + 4, :]),
                                                    sem=s_w, dma=True))
                thunks.append(lambda: load_gain(gpre, 8 + l, s_w))
                thunks.append(lambda: load_gain(gpost, 12 + l, s_w))
            return Wts

        def flush_w(Wts):
            while Wts["thunks"]:
                Wts["thunks"].pop(0)()

        def phase_mlp(l, src, dst, Wts):
            nonlocal junk
            TT = 256
            NS = TT // 128
            NTT = S // TT
            with ExitStack() as es:
                sb = lambda n, s, d: es.enter_context(nc.sbuf_tensor(un(n), s, d))
                junk = sb("junk", [128, D], BF16)
                flush_w(Wts)
                wup, wdn, gpre, gpost = Wts["wup"], Wts["wdn"], Wts["gpre"], Wts["gpost"]
                t_w = (Wts["sem"], Wts["sem"].n)
                XT = mkring(es, "xt", 6, [128, D], F32, sems=True)
                HB = mkring(es, "hb", 2, [128, D], BF16)
                T2 = mkring(es, "t2", 2, [128, D], F32, sems=True)
                R = mkring(es, "r", 3, [128, TT], BF16)
                hT = sb("hT", [128, KC, TT], BF16)
                uT = sb("uT", [128, FC, TT], BF16)
                TP = mkring(es, "tp", 2, [128, KC, 128], BF16, psum=True)
                UP = mkring(es, "up", 2, [128, 512], F32, psum=True)
                DN = mkring(es, "dn", 2, [128, D], F32, psum=True)
                xinfo = {}
                hT_rd = []
                uT_rd = []

                nds = {}

                def do_norm(t):
                    for s in range(NS):
                        rows = slice(t * TT + s * 128, t * TT + (s + 1) * 128)
                        nd = norm_part(src[rows, :], gpre, t_w, XT, HB)
                        nds[(t, s)] = nd
                        xinfo[(t, s)] = (nd["xi"], nd["xt"], nd["t_ld"])

                def do_T(t):
                    toks = []
                    for s in range(NS):
                        toks.append(T_part(nds.pop((t, s)), TP, HB, hT[:, :, s * 128:(s + 1) * 128], hT_rd))
                    return toks

                def do_up(t, t_hT):
                    last = None
                    tks = []
                    for fc in range(FC):
                        ui, up, uw = UP.acquire()

                        def f(e, fc=fc, up=up):
                            for kc in range(KC):
                                ins = e.matmul(up[:, 0:TT], lhsT=wup[:, kc, fc * 128:(fc + 1) * 128], rhs=hT[:, kc, :],
                                               start=(kc == 0), stop=(kc == KC - 1))
                            return ins
                        t_mm = add("pe", f, waits=list(t_hT) + [t_w] + uw)
                        ri, r, rw = R.acquire()
                        t_r = add("act", lambda e, r=r, up=up: e.activation(out=r[:], in_=up[:, 0:TT], func=AF.Relu), waits=[t_mm] + rw)
                        UP.release(ui, t_r)
                        t_u = add("dve", lambda e, fc=fc, r=r: e.tensor_tensor(out=uT[:, fc, :], in0=r[:], in1=r[:], op=ALU.mult),
                                  waits=[t_r] + (uT_rd if fc == 0 else []))
                        R.release(ri, t_u)
                        tks.append(t_u)
                        last = t_mm
                    return last, tks

                def do_down(t, t_us):
                    last = None
                    for s in range(NS):
                        rows = slice(t * TT + s * 128, t * TT + (s + 1) * 128)
                        di, dn, dw = DN.acquire()

                        def f(e, s=s, dn=dn):
                            for dh in range(2):
                                for fc in range(FC):
                                    ins = e.matmul(dn[:, dh * 512:(dh + 1) * 512], lhsT=uT[:, fc, s * 128:(s + 1) * 128],
                                                   rhs=wdn[:, fc, dh * 512:(dh + 1) * 512], start=(fc == 0), stop=(fc == FC - 1))
                            return ins
                        t_mm = add("pe", f, waits=[t_us[-1]] + dw)
                        xi, xt, t_ld = xinfo.pop((t, s))
                        t_t2, t_st, t_add = epilogue(dn[:], t_mm, gpost, t_w, xt, t_ld, T2, dst[rows, :])
                        DN.release(di, t_t2)
                        XT.release(xi, t_add)
                        last = t_mm
                    return last

                do_norm(0)
                t_hT = do_T(0)
                for t in range(NTT):
                    if t + 1 < NTT:
                        do_norm(t + 1)
                    t_upmm, t_us = do_up(t, t_hT)
                    hT_rd[:] = [t_upmm]
                    if t + 1 < NTT:
                        t_hT = do_T(t + 1)
                    t_dn = do_down(t, t_us)
                    uT_rd[:] = [t_dn]
                sc.barrier()
                sc.emit()

        def phase_proj(kind, j, src):
            nonlocal junk
            TT = 512
            NS = 4
            NTT = S // TT
            with ExitStack() as es:
                sb = lambda n, s, d: es.enter_context(nc.sbuf_tensor(un(n), s, d))
                junk = sb("junk", [128, D], BF16)
                s_w = namedsem("s_w")
                gpre = sb("gpre", [128, D], F32)
                if kind == "kv":
                    W = sb("wkv", [128, KC, 2 * D], BF16)
                    load_w_rows(W, w_kv, KC, 2 * D, s_w, 1)
                    t_w = load_gain(gpre, 16, s_w)
                    dstT = KTd
                else:
                    W = sb("wq", [128, KC, D], BF16)
                    load_w_rows(W, w_q[j], KC, D, s_w, 2)
                    t_w = load_gain(gpre, 2 + j, s_w)
                    dstT = QTd
                dstTv = dstT.rearrange("(h p) s -> p h s", p=128)
                XT = mkring(es, "xt", 3, [128, D], F32, sems=True)
                HB = mkring(es, "hb", 3, [128, D], BF16)
                HT = mkring(es, "hT", 2, [128, KC, TT], BF16)
                KS = mkring(es, "ks", 2, [128, KC, TT], BF16, sems=True)
                TP = mkring(es, "tp", 2, [128, KC, 128], BF16, psum=True)
                KP = mkring(es, "kp", 2, [128, 512], F32, psum=True)
                if kind == "kv":
                    VP = mkring(es, "vp", 2, [128, D], F32, psum=True)
                    VS = mkring(es, "vs", 3, [128, D], F32, sems=True)
                    VB = mkring(es, "vb", 2, [128, D], F32, sems=True)
                    DV = mkring(es, "dv", 2, [128, D], BF16, sems=True)
                    s_z = namedsem("s_vz")
                    add("sp", lambda e: e.dma_start(out=Vd[S:S + 1, 0:512], in_=zeros[0:1, :].bitcast(F32)), sem=s_z, dma=True)
                    vstate = dict(t_prev_vst=add("sp", lambda e: e.dma_start(out=Vd[S:S + 1, 512:1024], in_=zeros[0:1, :].bitcast(F32)), sem=s_z, dma=True))
                order = [(t, s_) for t in reversed(range(NTT)) for s_ in range(NS)]
                tiles = {}
                nds = {}

                def A(n):
                    t, s_ = order[n]
                    rows = slice(t * TT + s_ * 128, t * TT + (s_ + 1) * 128)
                    nd = norm_part(src[rows, :], gpre, t_w, XT, HB)
                    XT.release(nd["xi"], nd["t_h"])
                    nds[n] = nd

                def B(n):
                    t, s_ = order[n]
                    if s_ == 0:
                        hi_, hT, hw_ = HT.acquire()
                        tiles[t] = dict(hi=hi_, hT=hT, hw=hw_, toks=[], rd=[])
                    td = tiles[t]
                    t_ev = T_part(nds.pop(n), TP, HB, td["hT"][:, :, s_ * 128:(s_ + 1) * 128], td["hw"])
                    td["toks"].append(t_ev)

                def Cpart(t, q):
                    td = tiles[t]
                    hT, toks = td["hT"], td["toks"]
                    if q == 0:
                        ki, ks, kw = KS.acquire()
                        td.update(ki=ki, ks=ks, kw=kw)
                    ks = td["ks"]
                    for hp in (2 * q, 2 * q + 1):
                        pi, kp, pw_ = KP.acquire()

                        def f(e, hp=hp, kp=kp):
                            for kc in range(KC):
                                ins = e.matmul(kp[:], lhsT=W[:, kc, hp * 128:(hp + 1) * 128], rhs=hT[:, kc, :],
                                               start=(kc == 0), stop=(kc == KC - 1))
                            return ins
                        t_mm = add("pe", f, waits=toks + [t_w] + pw_)
                        t_kev = add("act", lambda e, hp=hp, kp=kp: e.activation(out=ks[:, hp, :], in_=kp[:], func=AF.Copy),
                                    waits=[t_mm] + (td["kw"] if hp == 0 else []))
                        KP.release(pi, t_kev)
                        td["rd"] = [t_mm]
                    if q == 3:
                        t_kst = add("pool", lambda e: e.dma_start(out=dstTv[:, :, t * TT:(t + 1) * TT], in_=ks[:]),
                                    waits=[t_kev], sem=KS.sems[td["ki"]], dma=True)
                        KS.release(td["ki"], t_kst)
                    if kind == "kv":
                        s_ = NS - 1 - q
                        p0 = t * TT + s_ * 128
                        vi, vp, vw = VP.acquire()

                        def f(e):
                            for dh in range(2):
                                for kc in range(KC):
                                    ins = e.matmul(vp[:, dh * 512:(dh + 1) * 512], lhsT=hT[:, kc, s_ * 128:(s_ + 1) * 128],
                                                   rhs=W[:, kc, D + dh * 512:D + (dh + 1) * 512], start=(kc == 0), stop=(kc == KC - 1))
                            return ins
                        t_mm = add("pe", f, waits=toks + [t_w] + vw)
                        td["rd"] = [t_mm]
                        si_, vs, sw_ = VS.acquire()
                        t_vev = add("act", lambda e: e.activation(out=vs[:], in_=vp[:], func=AF.Copy), waits=[t_mm] + sw_)
                        VP.release(vi, t_vev)
                        t_vst = add("pool", lambda e: e.dma_start(out=Vd[p0:p0 + 128, :], in_=vs[:]), waits=[t_vev],
                                    sem=VS.sems[si_], dma=True)
                        if vstate.get("pend") is not None:
                            vstate["pend"]()
                        prev_vst = vstate["t_prev_vst"]
                        vstate["t_prev_vst"] = t_vst

                        def diff(vs=vs, si_=si_, p0=p0, t_vst=t_vst, t_vev=t_vev, prev_vst=prev_vst):
                            bi, vb, bw = VB.acquire()
                            t_vb = add("pool", lambda e: e.dma_start(out=vb[:], in_=Vd[p0 + 1:p0 + 129, :]),
                                       waits=[t_vst, prev_vst] + bw, sem=VB.sems[bi], dma=True)
                            di, dv, dw = DV.acquire()
                            t_dv = add("dve", lambda e: e.tensor_tensor(out=dv[:], in0=vb[:], in1=vs[:], op=ALU.subtract),
                                       waits=[t_vb, t_vev] + dw)
                            VB.release(bi, t_dv)
                            VS.release(si_, t_dv, t_vst)
                            t_dst = add("pool", lambda e: e.dma_start(out=dVd[p0:p0 + 128, :], in_=dv[:]), waits=[t_dv],
                                        sem=DV.sems[di], dma=True)
                            DV.release(di, t_dst)
                        vstate["pend"] = diff
                    if q == 3:
                        HT.release(td["hi"], *td["rd"])

                NSUB = len(order)
                for n in range(NSUB + 1 + NS):
                    if n < NSUB:
                        A(n)
                    if 0 <= n - 1 < NSUB:
                        B(n - 1)
                    m = n - 1 - NS
                    if m >= 0 and m < NSUB:
                        tprev, q = order[m][0], m % NS
                        Cpart(tprev, q)
                if kind == "kv" and vstate.get("pend") is not None:
                    vstate["pend"]()
                sc.barrier()
                sc.emit()

        def phase_attn(Wts=None):
            with ExitStack() as es:
                KT = mkring(es, "KT", 2, [128, S], BF16, sems=True)
                QT = mkring(es, "QT", 2, [128, S], BF16, sems=True)
                DVt = mkring(es, "DVt", 2, [128, NT, 128], BF16, sems=True)
                VSH = mkring(es, "VSH", 2, [128, NT, 128], F32, sems=True)
                OSB = mkring(es, "OSB", 2, [128, NT, 128], BF16, sems=True)
                CW = 1024
                OM = mkring(es, "om", 3, [128, CW], F32)
                PR = mkring(es, "pr", 4, [128, CW], BF16)
                PTS = mkring(es, "pts", 3, [128, 8, 128], BF16)
                OTMP = mkring(es, "otmp", 2, [128, 128], F32)
                ZP = mkring(es, "zp", 2, [128, CW], F32, psum=True)
                PTP = mkring(es, "ptp", 2, [128, 8, 128], BF16, psum=True)
                OP = mkring(es, "op", 2, [128, 512], F32, psum=True)
                dVv = dVd.rearrange("(n p) d -> p n d", p=128)
                Vshv = Vd[1:S + 1, :].rearrange("(n p) d -> p n d", p=128)
                Odv = Od.rearrange("(n p) d -> p n d", p=128)
                items = []
                pairs = []

                def issue_pair_loads(hp):
                    pair = pairs[hp]
                    cols = pair["cols"]
                    ki, kt, kw = KT.acquire()
                    qi, qt, qw = QT.acquire()
                    di, dvt, dw = DVt.acquire()
                    vi, vsh, vw = VSH.acquire()
                    oi, osb, ow = OSB.acquire()
                    t_k = add("sp", lambda e: e.dma_start(out=kt[:], in_=KTd[cols, :]), waits=kw, sem=KT.sems[ki], dma=True)
                    t_q = add("sp", lambda e: e.dma_start(out=qt[:], in_=QTd[cols, :]), waits=qw, sem=QT.sems[qi], dma=True)
                    t_d = add("sp", lambda e: e.dma_start(out=dvt[:], in_=dVv[:, :, cols]), waits=dw, sem=DVt.sems[di], dma=True)
                    t_v = add("sp", lambda e: e.dma_start(out=vsh[:], in_=Vshv[:, :, cols]), waits=vw, sem=VSH.sems[vi], dma=True)
                    pair.update(kt=kt, qt=qt, dvt=dvt, vsh=vsh, osb=osb, t_k=t_k, t_q=t_q, t_d=t_d, t_v=t_v, ow=ow,
                                ki=ki, qi=qi, di=di, vi=vi, oi=oi)

                NPAIR = KC
                NQ = NT
                for hp in range(NPAIR):
                    pair = dict(hp=hp, cols=slice(hp * 128, (hp + 1) * 128))
                    pairs.append(pair)
                    for i in range(NT - NQ, NT):
                        grp = dict(pair=pair, i=i, n_left=0, first=True)
                        nch = (S - 128 * i + CW - 1) // CW
                        for c in range(nch):
                            k0 = 128 * i + CW * c
                            w = min(CW, S - k0)
                            for hd in range(2):
                                items.append(dict(pair=pair, grp=grp, i=i, c=c, k0=k0, w=w, hd=hd, last=(c == nch - 1)))
                                grp["n_left"] += 1
                issue_pair_loads(0)
                if NPAIR > 1:
                    issue_pair_loads(1)
                N = len(items)
                o_state = {}

                def st_QK(k):
                    it = items[k]
                    zi, zp, zw = ZP.acquire()
                    it["zi"], it["zp"] = zi, zp
                    hs = slice(it["hd"] * 64, it["hd"] * 64 + 64)
                    pr_, i, k0, w = it["pair"], it["i"], it["k0"], it["w"]
                    def f(e):
                        for j0 in range(0, w, 512):
                            wj = min(512, w - j0)
                            ins = e.matmul(zp[:, j0:j0 + wj], lhsT=pr_["qt"][hs, i * 128:(i + 1) * 128],
                                           rhs=pr_["kt"][hs, k0 + j0:k0 + j0 + wj], start=True, stop=True)
                        return ins
                    it["t_qk"] = add("pe", f, waits=[pr_["t_k"], pr_["t_q"]] + zw)

                def st_sig(k):
                    it = items[k]
                    mi, om, mw = OM.acquire()
                    it["mi"], it["om"] = mi, om
                    zp, w = it["zp"], it["w"]
                    def f(e):
                        for j0 in range(0, w, 512):
                            wj = min(512, w - j0)
                            ins = e.activation(out=om[:, j0:j0 + wj], in_=zp[:, j0:j0 + wj], func=AF.Sigmoid, scale=-0.125)
                        return ins
                    it["t_sig"] = add("act", f, waits=[it["t_qk"]] + mw)
                    ZP.release(it["zi"], it["t_sig"])

                def st_scan(k):
                    it = items[k]
                    pi, p, pw_ = PR.acquire()
                    it["pi"], it["p"] = pi, p
                    om, w = it["om"], it["w"]
                    if it["c"] == 0:
                        d1 = maskfirst[:, 0:w]
                        init = 1.0
                        cw = []
                    else:
                        prev = items[k - 2]
                        d1 = zeros[:, 0:w]
                        init = prev["p"][:, prev["w"] - 1:prev["w"]]
                        cw = [prev["t_scan"]]
                    it["t_scan"] = add("dve", lambda e: e.tensor_tensor_scan(out=p[:, 0:w], data0=om[:, 0:w], data1=d1, initial=init,
                                                                            op0=ALU.mult, op1=ALU.max),
                                       waits=[it["t_sig"]] + pw_ + cw)
                    OM.release(it["mi"], it["t_scan"])
                    if it["c"] > 0:
                        PR.release(items[k - 2]["pi"], it["t_scan"])
                    it["t_p"] = it["t_scan"]

                def st_T(k):
                    it = items[k]
                    ti, tp, tw = PTP.acquire()
                    it["ti"], it["tp"] = ti, tp
                    p, nsub = it["p"], it["w"] // 128

                    def f(e):
                        for sbk in range(nsub):
                            ins = e.transpose(out=tp[:, sbk, :], in_=p[:, sbk * 128:(sbk + 1) * 128], identity=ident[:])
                        return ins
                    it["t_T"] = add("pe", f, waits=[it["t_p"]] + tw)
                    PR.release(it["pi"], it["t_T"])

                def st_evac(k):
                    it = items[k]
                    si_, pts, sw_ = PTS.acquire()
                    it["si"], it["pts"] = si_, pts
                    tp, nsub = it["tp"], it["w"] // 128
                    if True:
                        t_e = add("act", lambda e: e.activation(out=pts[:, 0:nsub, :], in_=tp[:, 0:nsub, :], func=AF.Copy),
                                  waits=[it["t_T"]] + sw_)
                    else:
                        t_e = add("dve", lambda e: e.tensor_copy(out=pts[:, 0:nsub, :], in_=tp[:, 0:nsub, :]),
                                  waits=[it["t_T"]] + sw_)
                    toks = [t_e]
                    it["t_ev"] = toks
                    PTP.release(it["ti"], *toks)

                def st_AV(k):
                    it = items[k]
                    grp, pr_ = it["grp"], it["pair"]
                    gw = []
                    if grp["first"]:
                        grp["first"] = False
                        oi, op, ow = OP.acquire()
                        grp["oi"], grp["op"] = oi, op
                        gw = ow
                    op = grp["op"]
                    pts, nsub, hd, k0 = it["pts"], it["w"] // 128, it["hd"], it["k0"]
                    oc = slice(hd * 64, hd * 64 + 64)
                    c, last = it["c"], it["last"]

                    i_q = it["i"]

                    def f(e):
                        if c == 0:
                            e.matmul(op[:, oc], lhsT=negtri[:], rhs=pr_["dvt"][:, i_q, oc], start=(hd == 0), stop=False,
                                     skip_group_check=True)
                        for sbk in range(nsub):
                            ins = e.matmul(op[:, oc], lhsT=pts[:, sbk, :], rhs=pr_["dvt"][:, k0 // 128 + sbk, oc],
                                           start=False, stop=(last and sbk == nsub - 1 and hd == 1),
                                           skip_group_check=True)
                        return ins
                    t_av = add("pe", f, waits=it["t_ev"] + [pr_["t_d"]] + gw)
                    PTS.release(it["si"], t_av)
                    grp["n_left"] -= 1
                    if grp["n_left"] == 0:
                        i = it["i"]
                        osb, vsh = pr_["osb"], pr_["vsh"]
                        mi_, otmp, mw_ = OTMP.acquire()
                        t_oa = add("act", lambda e: e.activation(out=otmp[:], in_=op[:, 0:128], func=AF.Copy), waits=[t_av] + mw_)
                        OP.release(grp["oi"], t_oa)
                        t_o = add("pool", lambda e: e.tensor_tensor(out=osb[:, i, :], in0=otmp[:], in1=vsh[:, i, :], op=ALU.add),
                                  waits=[t_oa, pr_["t_v"]] + pr_["ow"])
                        OTMP.release(mi_, t_o)
                        if i == NT - 1:
                            cols = pr_["cols"]
                            t_st = add("sp", lambda e: e.dma_start(out=Odv[:, :, cols], in_=osb[:]), waits=[t_o],
                                       sem=OSB.sems[pr_["oi"]], dma=True)
                            OSB.release(pr_["oi"], t_st)
                            KT.release(pr_["ki"], t_av)
                            QT.release(pr_["qi"], t_av)
                            DVt.release(pr_["di"], t_av)
                            VSH.release(pr_["vi"], t_o)
                            if pr_["hp"] + 2 < NPAIR:
                                issue_pair_loads(pr_["hp"] + 2)
                            if Wts is not None and Wts["thunks"]:
                                Wts["thunks"].pop(0)()

                for n in range(N + 5):
                    if n < N:
                        st_QK(n)
                        st_sig(n)
                        st_scan(n)
                    if 0 <= n - 3 < N:
                        st_T(n - 3)
                        st_evac(n - 3)
                    if 0 <= n - 5 < N:
                        st_AV(n - 5)
                sc.barrier()
                sc.emit()

        def phase_oproj(j, src, dst, Wts=None):
            nonlocal junk
            with ExitStack() as es:
                sb = lambda n, s, d: es.enter_context(nc.sbuf_tensor(un(n), s, d))
                junk = sb("junk", [128, D], BF16)
                s_w = namedsem("s_w")
                wo = sb("wo", [128, KC, D], BF16)
                gpost = sb("gpost", [128, D], F32)
                load_w_rows(wo, w_o[j], KC, D, s_w, 2)
                t_w = load_gain(gpost, 4 + 2 + j, s_w)
                OB = mkring(es, "ob", 3, [128, D], BF16, sems=True)
                XR = mkring(es, "xr", 4, [128, D], F32, sems=True)
                OT = mkring(es, "ot", 3, [128, KC, 128], BF16)
                T2 = mkring(es, "t2", 2, [128, D], F32, sems=True)
                TP = mkring(es, "tp", 2, [128, KC, 128], BF16, psum=True)
                YP = mkring(es, "yp", 2, [128, D], F32, psum=True)
                st = [dict() for _ in range(NT)]

                def A(i):
                    d = st[i]
                    rows = slice(i * 128, (i + 1) * 128)
                    bi, ob, bw = OB.acquire()
                    t_ob = add("sp", lambda e: e.dma_start(out=ob[:], in_=Od[rows, :]), waits=bw, sem=OB.sems[bi], dma=True)
                    xi, xr, xw = XR.acquire()
                    t_xr = add("sp", lambda e: e.dma_start(out=xr[:], in_=src[rows, :]), waits=xw, sem=XR.sems[xi], dma=True)
                    ti, ot, tw = OT.acquire()
                    t_T, t_ev = transpose8(ob, t_ob, TP, ot[:], tw)
                    OB.release(bi, t_T)
                    d.update(rows=rows, xi=xi, xr=xr, t_xr=t_xr, ti=ti, ot=ot, t_ev=t_ev)

                def B(i):
                    d = st[i]
                    ot = d["ot"]
                    yi, yp, yw = YP.acquire()

                    def f(e):
                        for dh in range(2):
                            for kc in range(KC):
                                ins = e.matmul(yp[:, dh * 512:(dh + 1) * 512], lhsT=ot[:, kc, :], rhs=wo[:, kc, dh * 512:(dh + 1) * 512],
                                               start=(kc == 0), stop=(kc == KC - 1))
                        return ins
                    t_mm = add("pe", f, waits=[d["t_ev"], t_w] + yw)
                    OT.release(d["ti"], t_mm)
                    t_t2, t_st, t_add = epilogue(yp[:], t_mm, gpost, t_w, d["xr"], d["t_xr"], T2, dst[d["rows"], :])
                    YP.release(yi, t_t2)
                    XR.release(d["xi"], t_add)
                    if Wts is not None and Wts["thunks"]:
                        Wts["thunks"].pop(0)()

                for n in range(NT + 1):
                    if n < NT:
                        A(n)
                    if 0 <= n - 1 < NT:
                        B(n - 1)
                sc.barrier()
                sc.emit()

        junk = None
        sc.emit()
        def pool_mlp(l, src):
            with ExitStack() as wes:
                Wts = mlp_weights(wes, l)
                phase_pool(l, src, xs, Wts)
                phase_mlp(l, xs, xs, Wts)

        def attn_oproj_mlp(j):
            with ExitStack() as wes:
                Wts = mlp_weights(wes, 2 + j, "up")
                phase_attn(Wts)
                flush_w(Wts)
                with ExitStack() as wes2:
                    mlp_weights(wes2, 2 + j, "down", Wts)
                    phase_oproj(j, xs, xs, Wts)
                    phase_mlp(2 + j, xs, out if j == 1 else xs, Wts)

        plist = [lambda: pool_mlp(0, x_in), lambda: pool_mlp(1, xs), lambda: phase_proj("kv", 0, xs)]
        for j in range(2):
            plist += [lambda j=j: phase_proj("q", j, xs), lambda j=j: attn_oproj_mlp(j)]
        for ph in plist:
            ph()
        add("sp", lambda e: e.nop())
        add("act", lambda e: e.nop())
        sc.emit()
    return nc


def host_consts():
    ident = np.eye(128, dtype=np.float32)
    a = np.arange(128)
    maskfirst = np.zeros((128, 1024), np.float32)
    maskfirst[:, :128] = (a[None, :] <= a[:, None]).astype(np.float32)
    maskT = (a[:, None] > a[None, :]).astype(np.float32)
    poolB = np.zeros((12, 128, 128), np.float32)
    for g, w in enumerate(POOL_WINDOWS):
        for ao in range(128):
            for i in range(w):
                b = ao + i
                if b < 128:
                    poolB[g, b, ao] += 1.0 / w
                else:
                    poolB[4 + g, b - 128, ao] += 1.0 / w
            poolB[g, ao, ao] -= 1.0
            cnt = min(128 - ao, w)
            for i in range(cnt):
                poolB[8 + g, ao + i, ao] += 1.0 / cnt
            poolB[8 + g, ao, ao] -= 1.0
    return dict(c_ident=ident, c_maskfirst=maskfirst, c_maskT=maskT, c_negtri=np.ascontiguousarray(maskT - 1.0), c_poolB=poolB)


_NC_CACHE = {}


def run(x, pool_w, pool_scale, w_q, w_kv, kv_norm_g, w_o, w_up, w_down,
        mix_pre_g, mix_post_g, mlp_pre_g, mlp_post_g, n_cores=None):
    f = lambda a: np.ascontiguousarray(np.asarray(a, dtype=np.float32))
    x = f(x)
    B, S, _ = x.shape
    gains = np.concatenate([f(mix_pre_g), f(mix_post_g), f(mlp_pre_g), f(mlp_post_g), f(kv_norm_g)[None, :], f(pool_scale)], axis=0)
    shared = dict(gains=np.ascontiguousarray(gains), pool_w=f(pool_w), w_q=f(w_q), w_kv=f(w_kv), w_o=f(w_o),
                  w_up=f(w_up), w_down=f(w_down))
    shared.update(host_consts())
    if S not in _NC_CACHE:
        _NC_CACHE[S] = build(S)
    nc = _NC_CACHE[S]
    in_maps = []
    for b in range(B):
        m = dict(shared)
        m["x"] = np.ascontiguousarray(x[b, ::-1, :])
        in_maps.append(m)
    res = run_bass_kernel_spmd(nc, in_maps, core_ids=list(range(B)))
    outs = [np.asarray(r["out"], dtype=np.float32)[::-1, :] for r in res.results]
    return np.ascontiguousarray(np.stack(outs, axis=0))


def kernel(**inputs):
    return run(**inputs)
```
